# Optimizing a Trainium2 kernel written in Bass

```python
import math
import jax, jax.numpy as jnp
from jax import lax
import numpy as np

D_MODEL = 1024
BATCH = 16
SEQ = 2048
DEPTH = 2

PLE_DIM = 256
ROPE_THETA = 500000.0
Q_BLOCK = 128

MLA_HEADS = 8
MLA_Q_LORA = 256
MLA_KV_LORA = 128
MLA_NOPE = 64
MLA_ROPE = 32
MLA_V = 64
MOBA_HEADS = 8
MOBA_HEAD_DIM = 64
MOBA_ROT = MOBA_HEAD_DIM // 4
MOBA_BLOCK = 256
MOBA_TOPK = 3
MOBA_WIDTH = MOBA_HEADS * MOBA_HEAD_DIM
ATT_IN_DIM = MLA_Q_LORA + MLA_KV_LORA + MLA_ROPE + 3 * MOBA_WIDTH
ATT_OUT_IN = MLA_HEADS * MLA_V + MOBA_WIDTH
SSD_INNER = 2 * D_MODEL
SSD_HEAD_DIM = 64
SSD_HEADS = SSD_INNER // SSD_HEAD_DIM
SSD_GROUPS = 4
SSD_STATE = 128
SSD_CONV = 4
SSD_CHUNK = 128
SSD_CONV_DIM = SSD_INNER + 2 * SSD_GROUPS * SSD_STATE
SSD_IN_DIM = SSD_INNER + SSD_CONV_DIM + SSD_HEADS
D_FF = 2816
FFN_CONV = 3

LN_EPS = 1e-5
RMS_EPS = 1e-6
DEEPNORM_ALPHA = (2 * DEPTH) ** 0.25
DEEPNORM_BETA = (8 * DEPTH) ** -0.25

kernel_name = "hybrid_mla_moba_ssd_convffn_deepnorm"


def layer_norm(x, g, b):
    xf = x.astype(jnp.float32)
    mu = jnp.mean(xf, axis=-1, keepdims=True)
    var = jnp.mean(jnp.square(xf - mu), axis=-1, keepdims=True)
    return ((xf - mu) * lax.rsqrt(var + LN_EPS) * g.astype(jnp.float32) + b.astype(jnp.float32)).astype(x.dtype)


def rms_norm(x, g):
    xf = x.astype(jnp.float32)
    ms = jnp.mean(jnp.square(xf), axis=-1, keepdims=True)
    return (xf * lax.rsqrt(ms + RMS_EPS) * g.astype(jnp.float32)).astype(x.dtype)


def group_rms_norm(y, g):
    B, S, C = y.shape
    yg = y.reshape(B, S, SSD_GROUPS, C // SSD_GROUPS)
    yg = yg * lax.rsqrt(jnp.mean(jnp.square(yg), axis=-1, keepdims=True) + RMS_EPS)
    return yg.reshape(B, S, C) * g.astype(jnp.float32)


def rope_cos_sin(positions, rot_dim):
    inv_freq = ROPE_THETA ** (-jnp.arange(0, rot_dim, 2, dtype=jnp.float32) / rot_dim)
    ang = positions.astype(jnp.float32)[..., None] * inv_freq
    return jnp.cos(ang)[:, :, None, :], jnp.sin(ang)[:, :, None, :]


def apply_rope(x, cos, sin):
    xf = x.astype(jnp.float32)
    x1, x2 = jnp.split(xf, 2, axis=-1)
    return jnp.concatenate([x1 * cos - x2 * sin, x2 * cos + x1 * sin], axis=-1).astype(x.dtype)


def causal_depthwise_conv(x, w, b):
    k, c = w.shape
    y = lax.conv_general_dilated(x, w[:, None, :], window_strides=(1,), padding=[(k - 1, 0)],
                                 dimension_numbers=('NWC', 'WIO', 'NWC'), feature_group_count=c)
    return y + b


def to_query_blocks(t):
    B, S = t.shape[0], t.shape[1]
    return t.reshape(B, S // Q_BLOCK, Q_BLOCK, *t.shape[2:]).swapaxes(0, 1)


def from_query_blocks(o):
    o = o.swapaxes(0, 1)
    return o.reshape(o.shape[0], o.shape[1] * o.shape[2], *o.shape[3:])


def mla_attention(q, k, v):
    S, dqk = q.shape[1], q.shape[3]
    scale = dqk ** -0.5
    kpos = jnp.arange(S)

    def block(args):
        qb, bi = args
        qpos = bi * Q_BLOCK + jnp.arange(Q_BLOCK)
        s = jnp.einsum('bqhd,bkhd->bhqk', qb, k).astype(jnp.float32) * scale
        s = jnp.where(kpos[None, :] <= qpos[:, None], s, -jnp.inf)
        pr = jax.nn.softmax(s, axis=-1).astype(v.dtype)
        return jnp.einsum('bhqk,bkhd->bqhd', pr, v)

    out = lax.map(block, (to_query_blocks(q), jnp.arange(S // Q_BLOCK)))
    return from_query_blocks(out)


def moba_attention(q, k, v):
    B, S, H, dh = q.shape
    scale = dh ** -0.5
    nb = -(-S // MOBA_BLOCK)
    n_sel = min(MOBA_TOPK, nb)
    pad = nb * MOBA_BLOCK - S
    kp = jnp.pad(k, ((0, 0), (0, pad), (0, 0), (0, 0)))
    vp = jnp.pad(v, ((0, 0), (0, pad), (0, 0), (0, 0)))
    kb = kp.reshape(B, nb, MOBA_BLOCK, H, dh).transpose(0, 3, 1, 2, 4)
    vb = vp.reshape(B, nb, MOBA_BLOCK, H, dh).transpose(0, 3, 1, 2, 4)
    k_mean = jnp.mean(kb.astype(jnp.float32), axis=3)
    b_idx = jnp.arange(B)[:, None, None, None]
    h_idx = jnp.arange(H)[None, :, None, None]
    slots = jnp.arange(n_sel)

    def block(args):
        qb, bi = args
        q0 = bi * Q_BLOCK
        own = q0 // MOBA_BLOCK
        qpos = q0 + jnp.arange(Q_BLOCK)
        gate = jnp.einsum('bqhd,bhnd->bhqn', qb.astype(jnp.float32), k_mean)
        gate = jnp.where(jnp.arange(nb) < own, gate, -jnp.inf)
        _, sel = lax.top_k(gate, n_sel)
        k_sel = kb[b_idx, h_idx, sel]
        v_sel = vb[b_idx, h_idx, sel]
        s_sel = jnp.einsum('bqhd,bhqrkd->bhqrk', qb, k_sel).astype(jnp.float32) * scale
        s_sel = jnp.where((slots < own)[:, None], s_sel, -jnp.inf)
        k_own = lax.dynamic_slice_in_dim(kp, own * MOBA_BLOCK, MOBA_BLOCK, axis=1)
        v_own = lax.dynamic_slice_in_dim(vp, own * MOBA_BLOCK, MOBA_BLOCK, axis=1)
        s_own = jnp.einsum('bqhd,bkhd->bhqk', qb, k_own).astype(jnp.float32) * scale
        kpos = own * MOBA_BLOCK + jnp.arange(MOBA_BLOCK)
        s_own = jnp.where(kpos[None, :] <= qpos[:, None], s_own, -jnp.inf)
        n_g = n_sel * MOBA_BLOCK
        s_all = jnp.concatenate([s_sel.reshape(B, H, Q_BLOCK, n_g), s_own], axis=-1)
        pr = jax.nn.softmax(s_all, axis=-1).astype(v.dtype)
        p_sel = pr[..., :n_g].reshape(B, H, Q_BLOCK, n_sel, MOBA_BLOCK)
        p_own = pr[..., n_g:]
        return (jnp.einsum('bhqrk,bhqrkd->bqhd', p_sel, v_sel)
                + jnp.einsum('bhqk,bkhd->bqhd', p_own, v_own))

    out = lax.map(block, (to_query_blocks(q), jnp.arange(S // Q_BLOCK)))
    return from_query_blocks(out)


def hybrid_attention_mixer(x, cos_m, sin_m, cos_b, sin_b, w_in, q_norm, w_uq, kv_norm, w_ukv, w_out):
    B, S, _ = x.shape
    h = x @ w_in
    o1 = MLA_Q_LORA
    o2 = o1 + MLA_KV_LORA
    o3 = o2 + MLA_ROPE
    c_q, c_kv, k_rope, q_b, k_b, v_b = jnp.split(
        h, [o1, o2, o3, o3 + MOBA_WIDTH, o3 + 2 * MOBA_WIDTH], axis=-1)
    q = (rms_norm(c_q, q_norm) @ w_uq).reshape(B, S, MLA_HEADS, MLA_NOPE + MLA_ROPE)
    q = jnp.concatenate([q[..., :MLA_NOPE], apply_rope(q[..., MLA_NOPE:], cos_m, sin_m)], axis=-1)
    kv = (rms_norm(c_kv, kv_norm) @ w_ukv).reshape(B, S, MLA_HEADS, MLA_NOPE + MLA_V)
    k_r = apply_rope(k_rope[:, :, None, :], cos_m, sin_m)
    k = jnp.concatenate([kv[..., :MLA_NOPE], jnp.broadcast_to(k_r, (B, S, MLA_HEADS, MLA_ROPE))], axis=-1)
    o_mla = mla_attention(q, k, kv[..., MLA_NOPE:])
    def partial_rope(t):
        t = t.reshape(B, S, MOBA_HEADS, MOBA_HEAD_DIM)
        return jnp.concatenate([apply_rope(t[..., :MOBA_ROT], cos_b, sin_b), t[..., MOBA_ROT:]], axis=-1)
    o_moba = moba_attention(partial_rope(q_b), partial_rope(k_b),
                            v_b.reshape(B, S, MOBA_HEADS, MOBA_HEAD_DIM))
    o = jnp.concatenate([o_mla.reshape(B, S, -1), o_moba.reshape(B, S, -1)], axis=-1)
    return o @ w_out


def ssd_chunked_scan(x, dt, a, bm, cm):
    B, S, H, P = x.shape
    G, N = bm.shape[2], bm.shape[3]
    hg = H // G
    L = SSD_CHUNK
    nc = S // L

    def chunks(t):
        return t.reshape(B, nc, L, *t.shape[2:]).swapaxes(0, 1)

    xc = chunks(x.astype(jnp.float32).reshape(B, S, G, hg, P))
    dtc = chunks(dt.reshape(B, S, G, hg))
    bc = chunks(bm.astype(jnp.float32))
    cc = chunks(cm.astype(jnp.float32))
    a_g = a.reshape(G, hg)
    causal = jnp.tril(jnp.ones((L, L), dtype=bool))[None, :, :, None, None]

    def step(state, inp):
        xk, dtk, bk, ck = inp
        la = jnp.cumsum(dtk * a_g, axis=1)
        seg = la[:, :, None] - la[:, None, :]
        decay = jnp.exp(jnp.where(causal, seg, -jnp.inf))
        xdt = xk * dtk[..., None]
        cb = jnp.einsum('bign,bjgn->bijg', ck, bk)
        y = jnp.einsum('bijgh,bjghp->bighp', cb[..., None] * decay, xdt)
        y = y + jnp.einsum('bign,bghpn->bighp', ck, state) * jnp.exp(la)[..., None]
        to_end = jnp.exp(la[:, -1:] - la)
        state = (state * jnp.exp(la[:, -1])[..., None, None]
                 + jnp.einsum('bjgn,bjgh,bjghp->bghpn', bk, to_end, xdt))
        return state, y

    state0 = jnp.zeros((B, G, hg, P, N), jnp.float32)
    _, ys = lax.scan(step, state0, (xc, dtc, bc, cc))
    return ys.swapaxes(0, 1).reshape(B, S, H, P)


def ssd_mixer(x, w_in, conv_w, conv_b, dt_bias, a_log, d_skip, norm_w, w_out):
    B, S, _ = x.shape
    h = x @ w_in
    z, xbc, dt = jnp.split(h, [SSD_INNER, SSD_INNER + SSD_CONV_DIM], axis=-1)
    xbc = jax.nn.silu(causal_depthwise_conv(xbc, conv_w, conv_b))
    xs, bm, cm = jnp.split(xbc, [SSD_INNER, SSD_INNER + SSD_GROUPS * SSD_STATE], axis=-1)
    xs = xs.reshape(B, S, SSD_HEADS, SSD_HEAD_DIM)
    bm = bm.reshape(B, S, SSD_GROUPS, SSD_STATE)
    cm = cm.reshape(B, S, SSD_GROUPS, SSD_STATE)
    dt = jax.nn.softplus(dt.astype(jnp.float32) + dt_bias.astype(jnp.float32))
    a = -jnp.exp(a_log.astype(jnp.float32))
    y = ssd_chunked_scan(xs, dt, a, bm, cm)
    y = y + d_skip.astype(jnp.float32)[:, None] * xs.astype(jnp.float32)
    y = y.reshape(B, S, SSD_INNER) * jax.nn.silu(z.astype(jnp.float32))
    y = group_rms_norm(y, norm_w).astype(x.dtype)
    return y @ w_out


def conv_ffn(x, w_up, conv_w, conv_b, w_down):
    g, u = jnp.split(x @ w_up, 2, axis=-1)
    g = causal_depthwise_conv(g, conv_w, conv_b)
    return (jax.nn.gelu(g, approximate=False) * u) @ w_down


def setup_inputs(seed: int = 0) -> dict:
    key = jax.random.key(seed)
    ks = iter(jax.random.split(key, 40))
    ne = (DEPTH + 1) // 2
    no = DEPTH // 2

    def nrm(shape, scale):
        return jax.random.normal(next(ks), shape, jnp.float32) * scale

    def gain(shape):
        return 1.0 + nrm(shape, 0.02)

    dt0 = jnp.exp(jax.random.uniform(next(ks), (no, SSD_HEADS), jnp.float32)
                  * (math.log(0.1) - math.log(0.001)) + math.log(0.001))
    return {
        "x": nrm((BATCH, SEQ, D_MODEL), 1.0),
        "p": nrm((DEPTH, BATCH, SEQ, PLE_DIM), 1.0),
        "positions": jnp.broadcast_to(jnp.arange(SEQ, dtype=jnp.int32), (BATCH, SEQ)),
        "att_w_in": nrm((ne, D_MODEL, ATT_IN_DIM), D_MODEL ** -0.5),
        "mla_q_norm": gain((ne, MLA_Q_LORA)),
        "mla_w_uq": nrm((ne, MLA_Q_LORA, MLA_HEADS * (MLA_NOPE + MLA_ROPE)), MLA_Q_LORA ** -0.5),
        "mla_kv_norm": gain((ne, MLA_KV_LORA)),
        "mla_w_ukv": nrm((ne, MLA_KV_LORA, MLA_HEADS * (MLA_NOPE + MLA_V)), MLA_KV_LORA ** -0.5),
        "att_w_out": nrm((ne, ATT_OUT_IN, D_MODEL), ATT_OUT_IN ** -0.5 * DEEPNORM_BETA),
        "ssd_w_in": nrm((no, D_MODEL, SSD_IN_DIM), D_MODEL ** -0.5),
        "ssd_conv_w": nrm((no, SSD_CONV, SSD_CONV_DIM), SSD_CONV ** -0.5),
        "ssd_conv_b": nrm((no, SSD_CONV_DIM), 0.02),
        "ssd_dt_bias": dt0 + jnp.log(-jnp.expm1(-dt0)),
        "ssd_a_log": jnp.log(jax.random.uniform(next(ks), (no, SSD_HEADS), jnp.float32, 1.0, 16.0)),
        "ssd_d": gain((no, SSD_HEADS)),
        "ssd_norm": gain((no, SSD_INNER)),
        "ssd_w_out": nrm((no, SSD_INNER, D_MODEL), SSD_INNER ** -0.5 * DEEPNORM_BETA),
        "ln_mix_g": gain((DEPTH, D_MODEL)),
        "ln_mix_b": nrm((DEPTH, D_MODEL), 0.02),
        "ffn_w_up": nrm((DEPTH, D_MODEL, 2 * D_FF), D_MODEL ** -0.5),
        "ffn_conv_w": nrm((DEPTH, FFN_CONV, D_FF), FFN_CONV ** -0.5),
        "ffn_conv_b": nrm((DEPTH, D_FF), 0.02),
        "ffn_w_down": nrm((DEPTH, D_FF, D_MODEL), D_FF ** -0.5 * DEEPNORM_BETA),
        "ln_ffn_g": gain((DEPTH, D_MODEL)),
        "ln_ffn_b": nrm((DEPTH, D_MODEL), 0.02),
        "ple_w_gate": nrm((DEPTH, D_MODEL, D_MODEL), D_MODEL ** -0.5),
        "ple_w_proj": nrm((DEPTH, PLE_DIM, D_MODEL), PLE_DIM ** -0.5),
    }


def reference(x, p, positions, att_w_in, mla_q_norm, mla_w_uq, mla_kv_norm, mla_w_ukv, att_w_out,
              ssd_w_in, ssd_conv_w, ssd_conv_b, ssd_dt_bias, ssd_a_log, ssd_d, ssd_norm, ssd_w_out,
              ln_mix_g, ln_mix_b, ffn_w_up, ffn_conv_w, ffn_conv_b, ffn_w_down, ln_ffn_g, ln_ffn_b,
              ple_w_gate, ple_w_proj):
    cos_m, sin_m = rope_cos_sin(positions, MLA_ROPE)
    cos_b, sin_b = rope_cos_sin(positions, MOBA_ROT)
    for i in range(DEPTH):
        j = i // 2
        if i % 2 == 0:
            m = hybrid_attention_mixer(x, cos_m, sin_m, cos_b, sin_b, att_w_in[j], mla_q_norm[j],
                                       mla_w_uq[j], mla_kv_norm[j], mla_w_ukv[j], att_w_out[j])
        else:
            m = ssd_mixer(x, ssd_w_in[j], ssd_conv_w[j], ssd_conv_b[j], ssd_dt_bias[j],
                          ssd_a_log[j], ssd_d[j], ssd_norm[j], ssd_w_out[j])
        x = layer_norm(DEEPNORM_ALPHA * x + m, ln_mix_g[i], ln_mix_b[i])
        f = conv_ffn(x, ffn_w_up[i], ffn_conv_w[i], ffn_conv_b[i], ffn_w_down[i])
        x = layer_norm(DEEPNORM_ALPHA * x + f, ln_ffn_g[i], ln_ffn_b[i])
        x = x + jax.nn.sigmoid(x @ ple_w_gate[i]) * (p[i] @ ple_w_proj[i])
    return x
```

```python
import bisect
from contextlib import ExitStack
import numpy as np
import ml_dtypes
import concourse.bass as bass
import concourse.mybir as mybir
from concourse.bass_utils import run_bass_kernel_spmd

F32 = mybir.dt.float32
BF16 = mybir.dt.bfloat16
I32 = mybir.dt.int32
AF = mybir.ActivationFunctionType
ALU = mybir.AluOpType
AX = mybir.AxisListType

NCORES = 8
SPC = 2
T = 2048
D = 1024
NT = T // 128
LN_EPS = 1e-5
RMS_EPS = 1e-6
ALPHA = 4.0 ** 0.25
THETA = 500000.0
DFF = 2816
NFC = DFF // 128
NEGBIG = -30000.0

SAME_ENGINE_SYNC = True


class Prog:
    CE = ("pe", "act", "dve", "pool", "sp")

    def __init__(self, nc, n_dma_sems=40):
        self.nc = nc
        self.E = {"pe": nc.tensor, "act": nc.scalar, "dve": nc.vector, "pool": nc.gpsimd, "sp": nc.sync}
        self.stack = ExitStack()
        self.sem = {e: self.stack.enter_context(nc.semaphore("s_" + e)) for e in self.CE}
        self.cnt = {e: 0 for e in self.CE}
        self.il = {e: [] for e in self.CE}
        self.sig_idx = {e: [] for e in self.CE}
        self.sig_val = {e: [] for e in self.CE}
        self.waited = {}
        self.dsem = [self.stack.enter_context(nc.semaphore("d%d" % i)) for i in range(n_dma_sems)]
        self.dval = [0] * n_dma_sems
        self.dnext = 0
        self.dwaited = {}
        self.bs = {}
        self.ninstr = 0

    def _signal_value(self, eng, idx):
        si = self.sig_idx[eng]
        k = bisect.bisect_left(si, idx)
        if k < len(si):
            return self.sig_val[eng][k]
        last = len(self.il[eng]) - 1
        assert last >= idx
        self.cnt[eng] += 1
        self.il[eng][last].then_inc(self.sem[eng], 1)
        si.append(last)
        self.sig_val[eng].append(self.cnt[eng])
        return self.cnt[eng]

    def _wait(self, weng, tok):
        if tok is None:
            return
        if tok[0] == "c":
            _, eng, idx = tok
            if eng == weng and (eng == "pe" or not SAME_ENGINE_SYNC):
                return
            key = (weng, eng)
            v = self._signal_value(eng, idx)
            if self.waited.get(key, 0) >= v:
                return
            self.E[weng].wait_ge(self.sem[eng], v)
            self.waited[key] = v
        else:
            _, k, v = tok
            key = (weng, k)
            if self.dwaited.get(key, 0) >= v:
                return
            self.E[weng].wait_ge(self.dsem[k], v)
            self.dwaited[key] = v

    def _deps(self, weng, reads, writes):
        for key in reads:
            st = self.bs.get(key)
            if st is not None:
                self._wait(weng, st["w"])
        for key in writes:
            st = self.bs.get(key)
            if st is not None:
                self._wait(weng, st["w"])
                for r in st["r"]:
                    self._wait(weng, r)

    def _update(self, tok, reads, writes):
        for key in reads:
            st = self.bs.setdefault(key, {"w": None, "r": []})
            if tok[0] == "c":
                st["r"] = [r for r in st["r"] if not (r[0] == "c" and r[1] == tok[1])]
            st["r"].append(tok)
        for key in writes:
            self.bs[key] = {"w": tok, "r": []}

    def op(self, eng, fn, reads=(), writes=()):
        self._deps(eng, reads, writes)
        ins = fn(self.E[eng])
        self.il[eng].append(ins)
        tok = ("c", eng, len(self.il[eng]) - 1)
        self._update(tok, reads, writes)
        self.ninstr += 1
        return tok

    def dma(self, q, out, in_, reads=(), writes=(), slow=False):
        k = self.dnext
        self.dnext = (self.dnext + 1) % len(self.dsem)
        if self.dval[k] > 0:
            self._wait(q, ("d", k, self.dval[k]))
        self._deps(q, reads, writes)
        self.dval[k] += 16
        self.E[q].dma_start(out=out, in_=in_, allow_slow_non_contiguous=slow).then_inc(self.dsem[k], 16)
        tok = ("d", k, self.dval[k])
        self._update(tok, reads, writes)
        self.ninstr += 1
        return tok

    def barrier(self):
        for e in ("pe", "act", "dve", "pool"):
            if self.il[e]:
                self._wait("sp", ("c", e, len(self.il[e]) - 1))
        for k in range(len(self.dsem)):
            if self.dval[k] > 0:
                self._wait("sp", ("d", k, self.dval[k]))
        ins = self.E["sp"].nop()
        self.il["sp"].append(ins)
        tok = ("c", "sp", len(self.il["sp"]) - 1)
        for e in ("pe", "act", "dve", "pool"):
            self._wait(e, tok)
        self.bs = {}


class Rot:
    def __init__(self, n):
        self.n = n
        self.i = -1

    def next(self):
        self.i = (self.i + 1) % self.n
        return self.i


def _layer0_perms():
    o_cq, o_ckv, o_kr = 0, 256, 384
    o_qb = 416
    o_kb = o_qb + 512
    o_vb = o_kb + 512
    cols = []
    cols += list(range(o_cq, o_cq + 256))
    cols += list(range(o_ckv, o_ckv + 128))
    cols += list(range(o_kr, o_kr + 32))
    cols += list(range(o_kr + 16, o_kr + 32)) + list(range(o_kr, o_kr + 16))
    for base in (o_qb, o_kb):
        rope, sw, non = [], [], []
        for h in range(8):
            hb = base + h * 64
            rope += list(range(hb, hb + 16))
            sw += list(range(hb + 8, hb + 16)) + list(range(hb, hb + 8))
            non += list(range(hb + 16, hb + 64))
        cols += rope + sw + non
    cols += list(range(o_vb, o_vb + 512))
    w_in_perm = np.array(cols, dtype=np.int64)
    rope, sw, nope = [], [], []
    for h in range(8):
        hb = h * 96
        rope += list(range(hb + 64, hb + 96))
        sw += list(range(hb + 80, hb + 96)) + list(range(hb + 64, hb + 80))
        nope += list(range(hb, hb + 64))
    w_uq_perm = np.array(rope + sw + nope, dtype=np.int64)
    kn, vv = [], []
    for h in range(8):
        hb = h * 128
        kn += list(range(hb, hb + 64))
        vv += list(range(hb + 64, hb + 128))
    w_ukv_perm = np.array(kn + vv, dtype=np.int64)
    return w_in_perm, w_uq_perm, w_ukv_perm


def _consts():
    c = {}
    c["ident"] = np.eye(128, dtype=np.float32).astype(ml_dtypes.bfloat16)
    k = np.arange(128)[:, None]
    q = np.arange(128)[None, :]
    c["tri"] = (q >= k).astype(np.float32).astype(ml_dtypes.bfloat16)
    c["ones"] = np.ones((128, 128), dtype=np.float32).astype(ml_dtypes.bfloat16)
    inv_m = (THETA ** (-np.arange(0, 32, 2, dtype=np.float32) / np.float32(32))).astype(np.float32)
    inv_b = (THETA ** (-np.arange(0, 16, 2, dtype=np.float32) / np.float32(16))).astype(np.float32)
    r = np.arange(128)
    fm = inv_m[r % 16]
    sgm = np.where((r % 32) < 16, -1.0, 1.0)
    fb = inv_b[r % 8]
    sgb = np.where((r % 16) < 8, -1.0, 1.0)
    c["ropec"] = np.stack([fm, sgm, fb, sgb], axis=1).astype(np.float32)
    cd = np.zeros((4, 8, 64), dtype=np.float32)
    for oi, own in enumerate((4, 5, 6, 7)):
        for j in range(own):
            for j2 in range(own):
                if j != j2:
                    cd[oi, j2, j * 8 + j2] += 1.0
                    cd[oi, j, j * 8 + j2] -= 1.0
    c["cdiff"] = cd.transpose(1, 0, 2).copy()
    cs = np.zeros((64, 72), dtype=np.float32)
    for j in range(8):
        for j2 in range(8):
            cs[j * 8 + j2, 64 + j] = 1.0
    c["csum"] = cs.astype(ml_dtypes.bfloat16)
    ek = np.zeros((8, T), dtype=np.float32)
    for j in range(8):
        ek[j, j * 256:(j + 1) * 256] = 1.0
    c["ek"] = ek.astype(ml_dtypes.bfloat16)
    c["identf"] = np.eye(128, dtype=np.float32)
    jj = np.arange(128)[:, None]
    ii = np.arange(128)[None, :]
    c["U"] = (jj <= ii).astype(np.float32)
    c["onesf"] = np.ones((128, 128), dtype=np.float32)
    c["neg"] = np.where(ii >= jj, 0.0, -1.0e6).astype(np.float32)
    es = np.zeros((32, 32, 128), dtype=np.float32)
    for h in range(32):
        es[h, h, :] = 1.0
    c["esel"] = es
    return c


class Builder:
    def __init__(self, debug=False, stop_after=None, nseq=SPC, dbg_names=None, cut=0, start_at=None):
        self.start_at = start_at
        self.dbg_names = dbg_names
        self.cut = cut
        self.debug = debug
        self.stop_after = stop_after
        self.nseq = nseq
        self.nc = bass.Bass("TRN2", target_bir_lowering=False)
        self.P = None
        self.din = {}
        self.dbg_outs = []

    def inp(self, name, shape, dt=F32):
        t = self.nc.dram_tensor(name, list(shape), dt, kind="ExternalInput").ap()
        self.din[name] = t
        return t

    def scratch(self, name, shape, dt=BF16, dbg=False):
        dbg = dbg and self.debug and (self.dbg_names is None or name in self.dbg_names)
        kind = "ExternalOutput" if dbg else "Internal"
        t = self.nc.dram_tensor(name, list(shape), dt, kind=kind).ap()
        if dbg:
            self.dbg_outs.append(name)
        return t

    def _uniq(self, name):
        self._uid = getattr(self, "_uid", 0) + 1
        return "%s_u%d" % (name, self._uid)

    def sb(self, st, name, shape, dt):
        return st.enter_context(self.nc.sbuf_tensor(self._uniq(name), list(shape), dt))

    def ps(self, st, name, shape, dt):
        return st.enter_context(self.nc.psum_tensor(self._uniq(name), list(shape), dt))

    def cast_weight(self, name, src, rows, cols, chunk_rows=128):
        dst = self.scratch(name + "_b", [rows, cols], BF16)
        P = self.P
        for r0 in range(0, rows, chunk_rows):
            r1 = min(rows, r0 + chunk_rows)
            P.dma("pool", dst[r0:r1, :], src[r0:r1, :], reads=(), writes=[(name, r0)])
        self.wkeys[name] = [(name, r0) for r0 in range(0, rows, chunk_rows)]
        return dst

    def build(self):
        nc = self.nc
        NS = self.nseq
        self.wkeys = {}
        with ExitStack() as top:
            P = self.P = Prog(nc)
            top.enter_context(P.stack)
            x_in = self.inp("x", [NS, T, D])
            p_in = self.inp("p", [2, NS, T, 256])
            pos_in = self.inp("pos", [NS, T], I32)
            w_in0 = self.inp("w_in0", [D, 2240])
            w_uq = self.inp("w_uq", [256, 1024])
            w_ukv = self.inp("w_ukv", [128, 1024])
            w_out0 = self.inp("w_out0", [D, D])
            qn_in = self.inp("q_norm", [256])
            kvn_in = self.inp("kv_norm", [128])
            lnmg = self.inp("ln_mix_g", [2, D])
            lnmb = self.inp("ln_mix_b", [2, D])
            lnfg = self.inp("ln_ffn_g", [2, D])
            lnfb = self.inp("ln_ffn_b", [2, D])
            w_up = self.inp("ffn_w_up", [2, D, 2 * DFF])
            fcw = self.inp("ffn_conv_w", [2, 3, DFF])
            fcb = self.inp("ffn_conv_b", [2, DFF])
            w_down = self.inp("ffn_w_down", [2, DFF, D])
            w_pg = self.inp("ple_w_gate", [2, D, D])
            w_pp = self.inp("ple_w_proj", [2, 256, D])
            c_ident = self.inp("c_ident", [128, 128], BF16)
            c_tri = self.inp("c_tri", [128, 128], BF16)
            c_ones = self.inp("c_ones", [128, 128], BF16)
            c_ropec = self.inp("c_ropec", [128, 4])
            c_cdiff = self.inp("c_cdiff", [8, 4, 64])
            c_csum = self.inp("c_csum", [64, 72], BF16)
            c_ek = self.inp("c_ek", [8, T], BF16)
            c_identf = self.inp("c_identf", [128, 128])
            c_U = self.inp("c_U", [128, 128])
            c_onesf = self.inp("c_onesf", [128, 128])
            c_neg = self.inp("c_neg", [128, 128])
            c_esel = self.inp("c_esel", [32, 32, 128])
            ssd_w_in = self.inp("ssd_w_in", [D, 5152])
            ssd_w_out = self.inp("ssd_w_out", [2048, D])
            ssd_cw = self.inp("ssd_conv_w", [4, 3072])
            ssd_cb = self.inp("ssd_conv_b", [3072])
            ssd_dtb = self.inp("ssd_dt_bias", [32])
            ssd_alog = self.inp("ssd_a_log", [32])
            ssd_d = self.inp("ssd_d", [32])
            ssd_norm = self.inp("ssd_norm", [2048])
            out = nc.dram_tensor("out", [NS, T, D], F32, kind="ExternalOutput").ap()
            self.io = dict(x=x_in, p=p_in, pos=pos_in, out=out)

            self.ident = self.sb(top, "ident", [128, 128], BF16)
            self.tri = self.sb(top, "tri", [128, 128], BF16)
            self.ones = self.sb(top, "ones", [128, 128], BF16)
            self.ropec = self.sb(top, "ropec", [128, 4], F32)
            self.epsln = self.sb(top, "epsln", [128, 1], F32)
            self.epsrms = self.sb(top, "epsrms", [128, 1], F32)
            self.negpi = self.sb(top, "negpi", [128, 1], F32)
            P.dma("sp", self.ident[:], c_ident[:, :], writes=["ident"])
            P.dma("sp", self.tri[:], c_tri[:, :], writes=["tri"])
            P.dma("sp", self.ones[:], c_ones[:, :], writes=["ones"])
            P.dma("sp", self.ropec[:], c_ropec[:, :], writes=["ropec"])
            P.op("dve", lambda e: e.memset(self.epsln[:], LN_EPS), writes=["epsln"])
            P.op("dve", lambda e: e.memset(self.epsrms[:], RMS_EPS), writes=["epsrms"])
            P.op("dve", lambda e: e.memset(self.negpi[:], -float(np.pi)), writes=["negpi"])
            self.consts_dram = dict(cdiff=c_cdiff, csum=c_csum, ek=c_ek, identf=c_identf, U=c_U, onesf=c_onesf, neg=c_neg, esel=c_esel)
            self.ssd_small = dict(cw=ssd_cw, cb=ssd_cb, dtb=ssd_dtb, alog=ssd_alog, d=ssd_d, norm=ssd_norm)
            self.one1 = self.sb(top, "one1", [128, 1], F32)
            P.op("dve", lambda e: e.memset(self.one1[:], 1.0), writes=["one1"])

            self.wb = {}
            self.wb["w_in0"] = self.cast_weight("w_in0", w_in0, D, 2240)
            self.wb["w_uq"] = self.cast_weight("w_uq", w_uq, 256, 1024)
            self.wb["w_ukv"] = self.cast_weight("w_ukv", w_ukv, 128, 1024)
            self.wb["w_out0"] = self.cast_weight("w_out0", w_out0, D, D)
            for l in range(2):
                if l == 1:
                    self.wb["w_ssd_in"] = self.cast_weight("w_ssd_in", ssd_w_in, D, 5152)
                    self.wb["w_ssd_out"] = self.cast_weight("w_ssd_out", ssd_w_out, 2048, D)
                self.wb["w_up%d" % l] = self.cast_weight("w_up%d" % l, w_up[l], D, 2 * DFF)
                self.wb["w_down%d" % l] = self.cast_weight("w_down%d" % l, w_down[l], DFF, D)
                self.wb["w_pg%d" % l] = self.cast_weight("w_pg%d" % l, w_pg[l], D, D)
                self.wb["w_pp%d" % l] = self.cast_weight("w_pp%d" % l, w_pp[l], 256, D)
            self.small = dict(qn=qn_in, kvn=kvn_in, lnmg=lnmg, lnmb=lnmb, lnfg=lnfg, lnfb=lnfb,
                              fcw=fcw, fcb=fcb)

            S = self.S = {}
            S["mla_qT"] = self.scratch("mla_qT", [NS, 8, 96, T], dbg=True)
            S["mla_kT"] = self.scratch("mla_kT", [NS, 8, 96, T], dbg=True)
            S["mla_v"] = self.scratch("mla_v", [NS, T, 512], dbg=True)
            S["moba_qT"] = self.scratch("moba_qT", [NS, 8, 64, T], dbg=True)
            S["moba_kT"] = self.scratch("moba_kT", [NS, 8, 64, T], dbg=True)
            S["moba_v"] = self.scratch("moba_v", [NS, T, 512], dbg=True)
            S["moba_ks"] = self.scratch("moba_ks", [NS, 8, 64, 8], F32, dbg=True)
            S["oT"] = self.scratch("oT", [NS, D, T], dbg=True)
            S["x1"] = self.scratch("x1", [NS, T, D], F32, dbg=True)
            S["x1T"] = self.scratch("x1T", [NS, D, T], dbg=True)
            S["fT"] = self.scratch("fT", [NS, DFF, T], dbg=True)
            S["x3"] = self.scratch("x3", [NS, T, D], F32, dbg=True)
            S["x3T"] = self.scratch("x3T", [NS, D, T], dbg=True)

            S["xbcT"] = self.scratch("xbcT", [NS, 3072, T], dbg=True)
            S["xb_tm"] = self.scratch("xb_tm", [NS, T, 2560], dbg=True)
            S["zs_tm"] = self.scratch("zs_tm", [NS, T, 2048], dbg=True)
            S["dt_tm"] = self.scratch("dt_tm", [NS, T, 32], F32, dbg=True)
            S["yn_tm"] = self.scratch("yn_tm", [NS, T, 2048], dbg=True)
            stages = [
                ("init", lambda: None),
                ("projA", lambda: self.phase_projA()),
                ("attn", lambda: self.phase_attn()),
                ("outln", lambda: self.phase_outproj_ln(self.S["oT"], "w_out0", 8, self.io["x"], 0)),
                ("ffnup0", lambda: self.phase_ffn_up(0)),
                ("ffndn0", lambda: self.phase_ffn_down(0, self.S["x3"], self.S["x3T"])),
                ("ssdproj", lambda: self.phase_ssd_proj()),
                ("ssdscan", lambda: self.phase_ssd_scan()),
                ("outln1", lambda: self.phase_outproj_ln(self.S["yn_tm"], "w_ssd_out", 16, self.S["x3"], 1, src_tm=True)),
                ("ffnup1", lambda: self.phase_ffn_up(1)),
                ("ffndn1", lambda: self.phase_ffn_down(1, self.io["out"], None)),
            ]
            if self.start_at is not None:
                names = [n for n, _ in stages]
                stages = [stages[0]] + stages[names.index(self.start_at):]
            for name, fn in stages:
                fn()
                P.barrier()
                if self.stop_after == name:
                    break
        return nc

    def load_w(self, st, name, rows, cols, c0=0, ncols=None, q="sp"):
        P = self.P
        ncols = cols if ncols is None else ncols
        kc = rows // 128
        t = self.sb(st, "W_" + name + "_%d" % c0, [128, kc, ncols], BF16)
        src = self.wb[name]
        for c in range(kc):
            P.dma(q, t[:, c, :], src[c * 128:(c + 1) * 128, c0:c0 + ncols],
                  reads=[(name, c * 128)], writes=[("W_" + name, c0)])
        return t

    def bcast_row(self, st, name, src_row, n):
        t = self.sb(st, name, [128, n], F32)
        self.P.dma("sp", t[:], src_row.partition_broadcast(128), writes=[name])
        return t

    def phase_projA(self):
        P, nc, S = self.P, self.nc, self.S
        with ExitStack() as st:
            w_in = self.load_w(st, "w_in0", D, 2240)
            w_uq = self.load_w(st, "w_uq", 256, 1024)
            w_ukv = self.load_w(st, "w_ukv", 128, 1024)
            WIN, WUQ, WUKV = ("W_w_in0", 0), ("W_w_uq", 0), ("W_w_ukv", 0)
            gq = self.sb(st, "gq", [128, 2], F32)
            gkv = self.sb(st, "gkv", [128, 1], F32)
            P.dma("sp", gq[:], self.small["qn"].rearrange("(c p) -> p c", p=128), writes=["gq"], slow=True)
            P.dma("sp", gkv[:], self.small["kvn"].rearrange("(c p) -> p c", p=128), writes=["gkv"], slow=True)
            xT = self.sb(st, "xT", [128, 8, T], BF16)
            xb = [self.sb(st, "xb%d" % i, [128, D], BF16) for i in range(2)]
            posi = self.sb(st, "posi", [128, T], I32)
            ang = self.sb(st, "ang", [128, T], F32)
            tmpa = self.sb(st, "tmpa", [128, T], F32)
            ang2 = self.sb(st, "ang2", [128, T], F32)
            cosM = self.sb(st, "cosM", [128, T], F32)
            sinM = self.sb(st, "sinM", [128, T], F32)
            cosB = self.sb(st, "cosB", [128, T], F32)
            sinB = self.sb(st, "sinB", [128, T], F32)
            cqg = self.sb(st, "cqg", [128, 2, 512], BF16)
            ckvg = self.sb(st, "ckvg", [128, 512], BF16)
            sq = [self.sb(st, "sq%d" % i, [128, 512], BF16) for i in range(2)]
            rq = self.sb(st, "rq", [128, 512], F32)
            rkv = self.sb(st, "rkv", [128, 512], F32)
            rkvt = self.sb(st, "rkvt", [128, 4], F32)
            t1 = [self.sb(st, "t1_%d" % i, [128, 512], F32) for i in range(2)]
            t2 = [self.sb(st, "t2_%d" % i, [128, 512], F32) for i in range(2)]
            ob = [self.sb(st, "ob%d" % i, [128, 512], BF16) for i in range(4)]
            ksum = self.sb(st, "ksum", [128, 4, 8], F32)
            pA = [self.ps(st, "pA%d" % i, [128, 512], F32) for i in range(5)]
            pT = self.ps(st, "pT", [128, 1024], BF16)
            pS = self.ps(st, "pS", [128, 512], F32)
            pR = self.ps(st, "pR", [128, 4], F32)
            rA, rT1, rT2, rOB, rSQ = Rot(5), Rot(2), Rot(2), Rot(4), Rot(2)

            for s in range(self.nseq):
                for t in range(NT):
                    b = t % 2
                    P.dma("pool", xb[b][:], self.io["x"][s, t * 128:(t + 1) * 128, :], writes=[("xb", b)])
                    for c in range(8):
                        P.op("pe", lambda e, c=c, b=b: e.transpose(pT[:, c * 128:(c + 1) * 128],
                                                                   xb[b][:, c * 128:(c + 1) * 128], self.ident[:]),
                             reads=[("xb", b), "ident"], writes=["pT"])
                    P.op("dve", lambda e, t=t: e.tensor_copy(out=xT[:, :, t * 128:(t + 1) * 128],
                                                             in_=pT[:].rearrange("p (c t) -> p c t", c=8)),
                         reads=["pT"], writes=[("xT", t // 4)])
                if self.cut == 1:
                    continue
                P.dma("sp", posi[:], self.io["pos"][s].partition_broadcast(128), writes=["posi"])
                P.op("dve", lambda e: e.tensor_copy(out=ang[:], in_=posi[:]), reads=["posi"], writes=["ang"])
                TWO_PI = float(2 * np.pi)

                def sin_table(dst, dkey, shift):
                    src = tmpa
                    if shift != 0.0:
                        P.op("dve", lambda e: e.tensor_scalar(out=dst[:], in0=tmpa[:], scalar1=shift, scalar2=None, op0=ALU.add),
                             reads=["tmpa"], writes=[dkey])
                        src = dst
                    P.op("dve", lambda e: e.tensor_scalar(out=posi[:], in0=src[:], scalar1=1.0 / TWO_PI, scalar2=None, op0=ALU.mult),
                         reads=["tmpa", dkey], writes=["posi"])
                    P.op("dve", lambda e: e.tensor_scalar(out=ang2[:], in0=posi[:], scalar1=-TWO_PI, scalar2=None, op0=ALU.mult),
                         reads=["posi"], writes=["ang2"])
                    P.op("dve", lambda e: e.tensor_tensor(out=dst[:], in0=ang2[:], in1=src[:], op=ALU.add),
                         reads=["ang2", "tmpa", dkey], writes=[dkey])
                    P.op("dve", lambda e: e.tensor_scalar(out=dst[:], in0=dst[:], scalar1=-float(np.pi), scalar2=float(np.pi), op0=ALU.max, op1=ALU.min),
                         reads=[dkey], writes=[dkey])
                    P.op("act", lambda e: e.activation(out=dst[:], in_=dst[:], func=AF.Sin), reads=[dkey], writes=[dkey])

                for (fcol, scol, ct, stb, nm) in ((0, 1, cosM, sinM, "M"), (2, 3, cosB, sinB, "B")):
                    P.op("dve", lambda e, fcol=fcol: e.tensor_scalar(out=tmpa[:], in0=ang[:], scalar1=self.ropec[:, fcol:fcol + 1],
                                                                     scalar2=None, op0=ALU.mult),
                         reads=["ang", "ropec"], writes=["tmpa"])
                    sin_table(stb, "sin" + nm, 0.0)
                    P.op("dve", lambda e, stb=stb, scol=scol: e.tensor_scalar(out=stb[:], in0=stb[:], scalar1=self.ropec[:, scol:scol + 1],
                                                                              scalar2=None, op0=ALU.mult),
                         reads=["sin" + nm, "ropec"], writes=["sin" + nm])
                    sin_table(ct, "cos" + nm, float(np.pi / 2))
                P.op("dve", lambda e: e.memset(ksum[:], 0.0), writes=["ksum"])
                if self.cut == 2:
                    continue

                def fm_group(wt, wkey, kc, c0, m, rhs_fn, rkeys):
                    a = rA.next()
                    for c in range(kc):
                        P.op("pe", lambda e, c=c, a=a: e.matmul(pA[a][0:m, :], wt[:, c, c0:c0 + m], rhs_fn(c),
                                                               start=(c == 0), stop=(c == kc - 1)),
                             reads=[wkey] + rkeys, writes=[("pA", a)])
                    return a

                def store_rows(src_tile, src_key, row_map, tc):
                    for (r0, n, dst) in row_map:
                        P.dma("sp", dst, src_tile[r0:r0 + n, :], reads=[src_key], writes=[("scr", id(dst))])

                for tcn in range(4):
                    tsl = slice(tcn * 512, (tcn + 1) * 512)
                    xk = [("xT", tcn)]
                    xr = lambda c: xT[:, c, tsl]
                    for c2 in range(2):
                        a = fm_group(w_in, WIN, 8, c2 * 128, 128, xr, xk)
                        P.op("dve", lambda e, a=a, c2=c2: e.tensor_scalar(out=cqg[:, c2, :], in0=pA[a][:], scalar1=gq[:, c2:c2 + 1],
                                                                         scalar2=None, op0=ALU.mult),
                             reads=[("pA", a), "gq"], writes=[("cqg", c2)])
                        i = rSQ.next()
                        P.op("act", lambda e, a=a, i=i: e.activation(out=sq[i][:], in_=pA[a][:], func=AF.Square),
                             reads=[("pA", a), ("cqg", c2)], writes=[("sq", i)])
                        P.op("pe", lambda e, i=i, c2=c2: e.matmul(pS[:], self.ones[:], sq[i][:], start=(c2 == 0), stop=(c2 == 1)),
                             reads=[("sq", i), "ones"], writes=["pS"])
                    P.op("act", lambda e: e.activation(out=rq[:], in_=pS[:], func=AF.Sqrt, bias=self.epsrms[:], scale=1.0 / 256),
                         reads=["pS", "epsrms"], writes=["rq"])
                    P.op("dve", lambda e: e.reciprocal(out=rq[:], in_=rq[:]), reads=["rq"], writes=["rq"])
                    a = fm_group(w_in, WIN, 8, 256, 128, xr, xk)
                    P.op("dve", lambda e, a=a: e.tensor_scalar(out=ckvg[:], in0=pA[a][:], scalar1=gkv[:, 0:1], scalar2=None, op0=ALU.mult),
                         reads=[("pA", a), "gkv"], writes=["ckvg"])
                    i = rSQ.next()
                    P.op("act", lambda e, a=a, i=i: e.activation(out=sq[i][:], in_=pA[a][:], func=AF.Square),
                         reads=[("pA", a), "ckvg"], writes=[("sq", i)])
                    P.op("pe", lambda e, i=i: e.matmul(pS[:], self.ones[:], sq[i][:], start=True, stop=True),
                         reads=[("sq", i), "ones"], writes=["pS"])
                    for j in range(4):
                        P.op("pe", lambda e, i=i, j=j: e.matmul(pR[:, j:j + 1], sq[i][:, j * 128:(j + 1) * 128], self.ones[:, 0:1],
                                                               start=True, stop=True),
                             reads=[("sq", i), "ones"], writes=["pR"])
                    P.op("act", lambda e: e.activation(out=rkv[:], in_=pS[:], func=AF.Sqrt, bias=self.epsrms[:], scale=1.0 / 128),
                         reads=["pS", "epsrms"], writes=["rkv"])
                    P.op("dve", lambda e: e.reciprocal(out=rkv[:], in_=rkv[:]), reads=["rkv"], writes=["rkv"])
                    P.op("act", lambda e: e.activation(out=rkvt[:], in_=pR[:], func=AF.Sqrt, bias=self.epsrms[:], scale=1.0 / 128),
                         reads=["pR", "epsrms"], writes=["rkvt"])
                    P.op("dve", lambda e: e.reciprocal(out=rkvt[:], in_=rkvt[:]), reads=["rkvt"], writes=["rkvt"])

                    if self.cut == 3:
                        continue

                    def rope_evac(a_main, a_sw, m, cs, sn, cskey, snkey, scale_tile=None, scale_key=None, sw_off=0):
                        i1, i2 = rT1.next(), rT2.next()
                        P.op("dve", lambda e: e.tensor_tensor(out=t1[i1][0:m, :], in0=pA[a_main][0:m, :], in1=cs[0:m, tsl], op=ALU.mult),
                             reads=[("pA", a_main), cskey], writes=[("t1", i1)])
                        P.op("dve", lambda e: e.tensor_tensor(out=t2[i2][0:m, :], in0=pA[a_sw][sw_off:sw_off + m, :], in1=sn[0:m, tsl], op=ALU.mult),
                             reads=[("pA", a_sw), snkey], writes=[("t2", i2)])
                        o = rOB.next()
                        if scale_tile is None:
                            P.op("pool", lambda e: e.tensor_tensor(out=ob[o][0:m, :], in0=t1[i1][0:m, :], in1=t2[i2][0:m, :], op=ALU.add),
                                 reads=[("t1", i1), ("t2", i2)], writes=[("ob", o)])
                        else:
                            P.op("pool", lambda e: e.tensor_tensor(out=t1[i1][0:m, :], in0=t1[i1][0:m, :], in1=t2[i2][0:m, :], op=ALU.add),
                                 reads=[("t1", i1), ("t2", i2)], writes=[("t1", i1)])
                            P.op("pool", lambda e: e.tensor_tensor(out=ob[o][0:m, :], in0=t1[i1][0:m, :], in1=scale_tile[0:m, :], op=ALU.mult),
                                 reads=[("t1", i1), scale_key], writes=[("ob", o)])
                        return o, (i1 if scale_tile is not None else None)

                    a = fm_group(w_in, WIN, 8, 384, 64, xr, xk)
                    o, _ = rope_evac(a, a, 32, cosM, sinM, "cosM", "sinM", sw_off=32)
                    for h in range(8):
                        P.dma("sp", S["mla_kT"][s, h, 0:32, tsl], ob[o][0:32, :], reads=[("ob", o)], writes=[("mla_kT", s, h)])
                    if self.cut == 4:
                        continue
                    for qi, (base, dst, dkey) in enumerate(((448, S["moba_qT"], "moba_qT"), (448 + 640, S["moba_kT"], "moba_kT"))):
                        a_r = fm_group(w_in, WIN, 8, base, 128, xr, xk)
                        a_s = fm_group(w_in, WIN, 8, base + 128, 128, xr, xk)
                        o, _ = rope_evac(a_r, a_s, 128, cosB, sinB, "cosB", "sinB")
                        if qi == 1:
                            P.op("dve", lambda e, o=o: e.tensor_reduce(out=ksum[:, 0, tcn * 2:tcn * 2 + 2],
                                                                       in_=ob[o][:].rearrange("p (b t) -> p b t", b=2),
                                                                       axis=AX.X, op=ALU.add),
                                 reads=[("ob", o)], writes=["ksum"])
                        for h in range(8):
                            P.dma("sp", dst[s, h, 0:16, tsl], ob[o][h * 16:(h + 1) * 16, :], reads=[("ob", o)], writes=[(dkey, s, h)])
                        for g in range(3):
                            a = fm_group(w_in, WIN, 8, base + 256 + g * 128, 128, xr, xk)
                            o = rOB.next()
                            P.op("act", lambda e, a=a, o=o: e.copy(out=ob[o][:], in_=pA[a][:]), reads=[("pA", a)], writes=[("ob", o)])
                            if qi == 1:
                                P.op("dve", lambda e, o=o, g=g: e.tensor_reduce(out=ksum[:, 1 + g, tcn * 2:tcn * 2 + 2],
                                                                                in_=ob[o][:].rearrange("p (b t) -> p b t", b=2),
                                                                                axis=AX.X, op=ALU.add),
                                     reads=[("ob", o)], writes=["ksum"])
                            R0 = g * 128
                            r = 0
                            while r < 128:
                                h = (R0 + r) // 48
                                d0 = (R0 + r) % 48
                                n = min(48 - d0, 128 - r)
                                P.dma("sp", dst[s, h, 16 + d0:16 + d0 + n, tsl], ob[o][r:r + n, :], reads=[("ob", o)], writes=[(dkey, s, h)])
                                r += n
                    if self.cut == 5:
                        continue
                    for j in range(4):
                        tt = tcn * 4 + j
                        a = rA.next()
                        for c in range(8):
                            P.op("pe", lambda e, c=c, a=a, tt=tt: e.matmul(pA[a][:], xT[:, c, tt * 128:(tt + 1) * 128], w_in[:, c, 1728:2240],
                                                                          start=(c == 0), stop=(c == 7)),
                                 reads=[WIN] + xk, writes=[("pA", a)])
                        o = rOB.next()
                        P.op("act", lambda e, a=a, o=o: e.copy(out=ob[o][:], in_=pA[a][:]), reads=[("pA", a)], writes=[("ob", o)])
                        P.dma("sp", S["moba_v"][s, tt * 128:(tt + 1) * 128, :], ob[o][:], reads=[("ob", o)], writes=[("moba_v", s)])
                    if self.cut == 6:
                        continue
                    cq_r = lambda c: cqg[:, c, :]
                    cqk = [("cqg", 0), ("cqg", 1)]
                    for g in range(2):
                        a_r = fm_group(w_uq, WUQ, 2, g * 128, 128, cq_r, cqk)
                        a_s = fm_group(w_uq, WUQ, 2, 256 + g * 128, 128, cq_r, cqk)
                        o, _ = rope_evac(a_r, a_s, 128, cosM, sinM, "cosM", "sinM", scale_tile=rq, scale_key="rq")
                        for hh in range(4):
                            h = g * 4 + hh
                            P.dma("sp", S["mla_qT"][s, h, 0:32, tsl], ob[o][hh * 32:(hh + 1) * 32, :], reads=[("ob", o)], writes=[("mla_qT", s, h)])
                    for g in range(4):
                        a = fm_group(w_uq, WUQ, 2, 512 + g * 128, 128, cq_r, cqk)
                        o = rOB.next()
                        P.op("dve", lambda e, a=a, o=o: e.tensor_tensor(out=ob[o][:], in0=pA[a][:], in1=rq[:], op=ALU.mult),
                             reads=[("pA", a), "rq"], writes=[("ob", o)])
                        for hh in range(2):
                            h = g * 2 + hh
                            P.dma("sp", S["mla_qT"][s, h, 32:96, tsl], ob[o][hh * 64:(hh + 1) * 64, :], reads=[("ob", o)], writes=[("mla_qT", s, h)])
                    if self.cut == 7:
                        continue
                    for g in range(4):
                        a = fm_group(w_ukv.rearrange("p (c n) -> p c n", c=1) if False else w_ukv, WUKV, 1, g * 128, 128,
                                     lambda c: ckvg[:], ["ckvg"])
                        o = rOB.next()
                        P.op("dve", lambda e, a=a, o=o: e.tensor_tensor(out=ob[o][:], in0=pA[a][:], in1=rkv[:], op=ALU.mult),
                             reads=[("pA", a), "rkv"], writes=[("ob", o)])
                        for hh in range(2):
                            h = g * 2 + hh
                            P.dma("sp", S["mla_kT"][s, h, 32:96, tsl], ob[o][hh * 64:(hh + 1) * 64, :], reads=[("ob", o)], writes=[("mla_kT", s, h)])
                    for j in range(4):
                        tt = tcn * 4 + j
                        a = rA.next()
                        P.op("pe", lambda e, a=a, j=j: e.matmul(pA[a][:], ckvg[:, j * 128:(j + 1) * 128], w_ukv[:, 0, 512:1024], start=True, stop=True),
                             reads=[WUKV, "ckvg"], writes=[("pA", a)])
                        o = rOB.next()
                        P.op("dve", lambda e, a=a, o=o, j=j: e.tensor_scalar(out=ob[o][:], in0=pA[a][:], scalar1=rkvt[:, j:j + 1], scalar2=None, op0=ALU.mult),
                             reads=[("pA", a), "rkvt"], writes=[("ob", o)])
                        P.dma("sp", S["mla_v"][s, tt * 128:(tt + 1) * 128, :], ob[o][:], reads=[("ob", o)], writes=[("mla_v", s)])
                for h in range(8):
                    P.dma("sp", S["moba_ks"][s, h, 0:16, :], ksum[h * 16:(h + 1) * 16, 0, :], reads=["ksum"], writes=[("moba_ks", s)])
                for g in range(3):
                    R0 = g * 128
                    r = 0
                    while r < 128:
                        h = (R0 + r) // 48
                        d0 = (R0 + r) % 48
                        n = min(48 - d0, 128 - r)
                        P.dma("sp", S["moba_ks"][s, h, 16 + d0:16 + d0 + n, :], ksum[r:r + n, 1 + g, :], reads=["ksum"], writes=[("moba_ks", s)])
                        r += n
            P.barrier()

    def phase_attn(self):
        P, S = self.P, self.S
        with ExitStack() as st:
            qa = [self.sb(st, "qa%d" % i, [128, T], BF16) for i in range(2)]
            ka = [self.sb(st, "ka%d" % i, [128, T], BF16) for i in range(2)]
            va = [self.sb(st, "va%d" % i, [128, NT, 128], BF16) for i in range(2)]
            ptl = [self.sb(st, "ptl%d" % i, [128, 512], BF16) for i in range(4)]
            rden = [self.sb(st, "rden%d" % i, [64, 512], F32) for i in range(2)]
            osb = [self.sb(st, "osb%d" % i, [64, 512], BF16) for i in range(2)]
            kms = [self.sb(st, "kms%d" % i, [64, 8], BF16) for i in range(2)]
            gsb = self.sb(st, "gsb", [8, 256], F32)
            ind = self.sb(st, "ind", [64, 256], BF16)
            cdiff = self.sb(st, "cdiff", [8, 4, 64], F32)
            csum = self.sb(st, "csum", [64, 72], BF16)
            pS = [self.ps(st, "pS%d" % i, [128, 512], F32) for i in range(3)]
            pO = [self.ps(st, "pO%d" % i, [128, 512], F32) for i in range(2)]
            pG = [self.ps(st, "pG%d" % i, [128, 512], F32) for i in range(3)]
            rS, rP, rO, rR = Rot(3), Rot(4), Rot(2), Rot(2)
            P.dma("sp", cdiff[:], self.consts_dram["cdiff"][:, :, :], writes=["cdiff"])
            P.dma("sp", csum[:], self.consts_dram["csum"][:, :], writes=["csum"])
            for i in range(2):
                P.op("dve", lambda e, i=i: e.memset(va[i][:, :, 64:128], 1.0), writes=[("va1", i)])
            hb = 0
            for kind in ("mla", "moba"):
                Kd = 96 if kind == "mla" else 72
                nload = 96 if kind == "mla" else 64
                scale = 96 ** -0.5 if kind == "mla" else 0.125
                qsrc, ksrc, vsrc = (S["mla_qT"], S["mla_kT"], S["mla_v"]) if kind == "mla" else (S["moba_qT"], S["moba_kT"], S["moba_v"])
                row0 = 0 if kind == "mla" else 512
                if kind == "moba":
                    P.barrier()
                    for i in range(2):
                        P.op("dve", lambda e, i=i: e.memset(qa[i][64:72, 0:1024], 0.0), writes=[("qa", i)])
                        P.dma("sp", ka[i][64:72, :], self.consts_dram["ek"][:, :], writes=[("ka", i)])
                for s in range(self.nseq):
                    for h in range(8):
                        b = hb % 2
                        hb += 1
                        P.dma("sp", qa[b][0:nload, :], qsrc[s, h, :, :], writes=[("qa", b)])
                        P.dma("sp", ka[b][0:nload, :], ksrc[s, h, :, :], writes=[("ka", b)])
                        P.dma("sp", va[b][:, :, 0:64], vsrc[s].rearrange("(n p) d -> p n d", p=128)[:, :, h * 64:(h + 1) * 64],
                              writes=[("va", b)])
                        if kind == "moba":
                            P.dma("pool", kms[b][:], S["moba_ks"][s, h, :, :], writes=[("kms", b)])
                            for qb in range(4, 8):
                                own, oi = qb, qb - 4
                                qsl = slice(qb * 256, (qb + 1) * 256)
                                P.op("pe", lambda e: e.matmul(pG[0][0:own, 0:256], kms[b][:, 0:own], qa[b][0:64, qsl], start=True, stop=True),
                                     reads=[("kms", b), ("qa", b)], writes=[("pG", 0)])
                                P.op("dve", lambda e: e.tensor_copy(out=gsb[0:own, :], in_=pG[0][0:own, 0:256]),
                                     reads=[("pG", 0)], writes=["gsb"])
                                P.op("pe", lambda e: e.matmul(pG[1][0:64, 0:256], cdiff[0:own, oi, :], gsb[0:own, :], start=True, stop=True),
                                     reads=["cdiff", "gsb"], writes=[("pG", 1)])
                                P.op("dve", lambda e: e.tensor_single_scalar(out=ind[:], in_=pG[1][0:64, 0:256], scalar=0.0, op=ALU.is_gt),
                                     reads=[("pG", 1)], writes=["ind"])
                                P.op("pe", lambda e: e.matmul(pG[2][0:72, 0:256], csum[:, :], ind[:], start=True, stop=True),
                                     reads=["csum", "ind"], writes=[("pG", 2)])
                                P.op("dve", lambda e: e.tensor_scalar(out=qa[b][64:72, qsl], in0=pG[2][64:72, 0:256], scalar1=2.5, scalar2=NEGBIG,
                                                                      op0=ALU.is_gt, op1=ALU.mult),
                                     reads=[("pG", 2)], writes=[("qa", b)])
                        for qc in range(4):
                            o = rO.next()
                            nk = 4 * qc + 4
                            for kt in range(nk):
                                i = kt - 4 * qc
                                c0 = 128 * i if i > 0 else 0
                                sp = rS.next()
                                P.op("pe", lambda e: e.matmul(pS[sp][:, c0:512], ka[b][0:Kd, kt * 128:(kt + 1) * 128],
                                                              qa[b][0:Kd, qc * 512 + c0:(qc + 1) * 512], start=True, stop=True),
                                     reads=[("ka", b), ("qa", b)], writes=[("pS", sp)])
                                pt = rP.next()
                                P.op("act", lambda e: e.activation(out=ptl[pt][:, c0:512], in_=pS[sp][:, c0:512], func=AF.Exp, scale=scale),
                                     reads=[("pS", sp)], writes=[("pt", pt)])
                                if i >= 0:
                                    P.op("pool", lambda e: e.tensor_tensor(out=ptl[pt][:, c0:c0 + 128], in0=ptl[pt][:, c0:c0 + 128],
                                                                           in1=self.tri[:], op=ALU.mult),
                                         reads=[("pt", pt), "tri"], writes=[("pt", pt)])
                                P.op("pe", lambda e: e.matmul(pO[o][:, c0:512], va[b][:, kt, :], ptl[pt][:, c0:512],
                                                              start=(kt == 0), stop=(kt == nk - 1)),
                                     reads=[("va", b), ("va1", b), ("pt", pt)], writes=[("pO", o)])
                            r = rR.next()
                            P.op("dve", lambda e: e.reciprocal(out=rden[r][:], in_=pO[o][64:128, :]), reads=[("pO", o)], writes=[("rden", r)])
                            P.op("dve", lambda e: e.tensor_tensor(out=osb[r][:], in0=pO[o][0:64, :], in1=rden[r][:], op=ALU.mult),
                                 reads=[("pO", o), ("rden", r)], writes=[("osb", r)])
                            P.dma("sp", S["oT"][s, row0 + h * 64:row0 + (h + 1) * 64, qc * 512:(qc + 1) * 512], osb[r][:],
                                  reads=[("osb", r)], writes=[("oT", s)])

    def ln_epilogue(self, tl, pm, xr, xr_key, g_bc, b_bc, keys, dst_tm, dst_fm, need_fm=False):
        P = self.P
        y, yn, junk, st4, ynb, xtt, pT = tl["y"], tl["yn"], tl["junk"], tl["st"], tl["ynb"], tl["xtt"], tl["pT"]
        P.op("dve", lambda e: e.scalar_tensor_tensor(out=y[:], in0=xr[:], scalar=ALPHA, in1=pm[:], op0=ALU.mult, op1=ALU.add),
             reads=[xr_key] + keys, writes=["ln_y"])
        P.op("dve", lambda e: e.memset(st4[:], 0.0), writes=["ln_st"])
        P.op("act", lambda e: e.activation(out=junk[:], in_=y[:], func=AF.Copy, accum_out=st4[:, 0:1]), reads=["ln_y", "ln_st"], writes=["ln_junk", "ln_st"])
        P.op("act", lambda e: e.activation(out=junk[:], in_=y[:], func=AF.Square, accum_out=st4[:, 1:2]), reads=["ln_y", "ln_st"], writes=["ln_junk", "ln_st"])
        P.op("dve", lambda e: e.tensor_scalar(out=st4[:, 0:2], in0=st4[:, 0:2], scalar1=1.0 / D, scalar2=None, op0=ALU.mult), reads=["ln_st"], writes=["ln_st"])
        P.op("dve", lambda e: e.tensor_tensor(out=st4[:, 2:3], in0=st4[:, 0:1], in1=st4[:, 0:1], op=ALU.mult), reads=["ln_st"], writes=["ln_st"])
        P.op("dve", lambda e: e.tensor_tensor(out=st4[:, 1:2], in0=st4[:, 1:2], in1=st4[:, 2:3], op=ALU.subtract), reads=["ln_st"], writes=["ln_st"])
        P.op("act", lambda e: e.activation(out=st4[:, 1:2], in_=st4[:, 1:2], func=AF.Sqrt, bias=self.epsln[:], scale=1.0), reads=["ln_st", "epsln"], writes=["ln_st"])
        P.op("dve", lambda e: e.reciprocal(out=st4[:, 1:2], in_=st4[:, 1:2]), reads=["ln_st"], writes=["ln_st"])
        P.op("dve", lambda e: e.scalar_tensor_tensor(out=st4[:, 3:4], in0=st4[:, 0:1], scalar=-1.0, in1=st4[:, 1:2], op0=ALU.mult, op1=ALU.mult),
             reads=["ln_st"], writes=["ln_st"])
        P.op("act", lambda e: e.activation(out=yn[:], in_=y[:], func=AF.Identity, scale=st4[:, 1:2], bias=st4[:, 3:4]),
             reads=["ln_y", "ln_st"], writes=["ln_yn"])
        P.op("dve", lambda e: e.tensor_tensor(out=yn[:], in0=yn[:], in1=g_bc[:], op=ALU.mult), reads=["ln_yn", "ln_g"], writes=["ln_yn"])
        P.op("pool", lambda e: e.tensor_tensor(out=yn[:], in0=yn[:], in1=b_bc[:], op=ALU.add), reads=["ln_yn", "ln_b"], writes=["ln_yn"])
        if dst_tm is not None:
            P.dma("sp", dst_tm, yn[:], reads=["ln_yn"], writes=[("dst_tm", id(dst_tm))])
        if dst_fm is not None or need_fm:
            self.to_fm(yn, "ln_yn", ynb, xtt, pT, dst_fm)

    def to_fm(self, src_f32, src_key, ynb, xtt, pT, dst_fm):
        P = self.P
        P.op("act", lambda e: e.copy(out=ynb[:], in_=src_f32[:]), reads=[src_key], writes=["ln_ynb"])
        for c in range(8):
            P.op("pe", lambda e, c=c: e.transpose(pT[:, c * 128:(c + 1) * 128], ynb[:, c * 128:(c + 1) * 128], self.ident[:]),
                 reads=["ln_ynb", "ident"], writes=["ln_pT"])
        P.op("dve", lambda e: e.tensor_copy(out=xtt[:], in_=pT[:].rearrange("p (c t) -> p c t", c=8)), reads=["ln_pT"], writes=["ln_xtt"])
        if dst_fm is not None:
            P.dma("sp", dst_fm, xtt[:], reads=["ln_xtt"], writes=[("dst_fm", id(dst_fm))])

    def ln_tiles(self, st):
        return dict(y=self.sb(st, "ln_y", [128, D], F32), yn=self.sb(st, "ln_yn", [128, D], F32),
                    junk=self.sb(st, "ln_junk", [128, D], F32), st=self.sb(st, "ln_st", [128, 4], F32),
                    ynb=self.sb(st, "ln_ynb", [128, D], BF16), xtt=self.sb(st, "ln_xtt", [128, 8, 128], BF16),
                    pT=self.ps(st, "ln_pT", [128, 1024], BF16))

    def phase_outproj_ln(self, oT, wname, kc, resid, l, src_tm=False):
        P, S = self.P, self.S
        with ExitStack() as st:
            if src_tm:
                ytm = [self.sb(st, "ytm%d" % i, [128, kc * 128], BF16) for i in range(2)]
                pTs = self.ps(st, "pTs", [128, kc * 128], BF16)
            W = self.load_w(st, wname, kc * 128, D)
            WK = ("W_" + wname, 0)
            g_bc = self.bcast_row(st, "ln_g", self.small["lnmg"][l], D)
            b_bc = self.bcast_row(st, "ln_b", self.small["lnmb"][l], D)
            tl = self.ln_tiles(st)
            ot = [self.sb(st, "ot%d" % i, [128, kc, 128], BF16) for i in range(2)]
            xr = [self.sb(st, "xr%d" % i, [128, D], F32) for i in range(2)]
            pm = [self.ps(st, "pm%d" % i, [128, D], F32) for i in range(2)]
            for s in range(self.nseq):
                src = None if src_tm else oT[s].rearrange("(c p) t -> p c t", p=128)
                x1T = S["x1T"][s].rearrange("(c p) t -> p c t", p=128)
                for t in range(NT):
                    b = t % 2
                    tsl = slice(t * 128, (t + 1) * 128)
                    if src_tm:
                        P.dma("sp", ytm[b][:], oT[s, tsl, :], writes=[("ytm", b)])
                        for c in range(kc):
                            P.op("pe", lambda e, c=c: e.transpose(pTs[:, c * 128:(c + 1) * 128], ytm[b][:, c * 128:(c + 1) * 128], self.ident[:]),
                                 reads=[("ytm", b), "ident"], writes=["pTs"])
                        P.op("dve", lambda e: e.tensor_copy(out=ot[b][:], in_=pTs[:].rearrange("p (c t) -> p c t", c=kc)),
                             reads=["pTs"], writes=[("ot", b)])
                    else:
                        P.dma("sp", ot[b][:], src[:, :, tsl], writes=[("ot", b)])
                    P.dma("sp", xr[b][:], resid[s, tsl, :], writes=[("xr", b)])
                    for half in range(2):
                        for c in range(kc):
                            P.op("pe", lambda e, c=c, half=half: e.matmul(pm[b][:, half * 512:(half + 1) * 512], ot[b][:, c, :],
                                                                          W[:, c, half * 512:(half + 1) * 512], start=(c == 0), stop=(c == kc - 1)),
                                 reads=[("ot", b), WK], writes=[("pm", b)])
                    self.ln_epilogue(tl, pm[b], xr[b], ("xr", b), g_bc, b_bc, [("pm", b)], S["x1"][s, tsl, :], x1T[:, :, tsl])

    def phase_ffn_up(self, l, xT_src=None):
        P, S = self.P, self.S
        xT_src = S["x1T"] if xT_src is None else xT_src
        wname = "w_up%d" % l
        with ExitStack() as st:
            x1h = self.sb(st, "x1h", [128, 8, 1024], BF16)
            cw = self.sb(st, "cw", [128, NFC, 3], F32)
            cb = self.sb(st, "cb", [128, NFC], F32)
            cprev = self.sb(st, "cprev", [128, NFC, 2], F32)
            for k in range(3):
                P.dma("sp", cw[:, :, k], self.small["fcw"][l, k].rearrange("(c p) -> p c", p=128), writes=["cw"], slow=True)
            P.dma("sp", cb[:], self.small["fcb"][l].rearrange("(c p) -> p c", p=128), writes=["cb"], slow=True)
            wg = [self.sb(st, "wg%d" % i, [128, 8, 256], BF16) for i in range(3)]
            gc = [self.sb(st, "gc%d" % i, [128, 1024], F32) for i in range(2)]
            gg = [self.sb(st, "gg%d" % i, [128, 1024], F32) for i in range(2)]
            fo = [self.sb(st, "fo%d" % i, [128, 1024], BF16) for i in range(2)]
            pg = [self.ps(st, "pg%d" % i, [128, 1024], F32) for i in range(2)]
            pu = [self.ps(st, "pu%d" % i, [128, 1024], F32) for i in range(2)]
            wsrc = self.wb[wname].rearrange("(k p) n -> p k n", p=128)
            rW, r2 = Rot(3), Rot(2)
            for s in range(self.nseq):
                xs = xT_src[s].rearrange("(c p) t -> p c t", p=128)
                for hf in range(2):
                    hsl = slice(hf * 1024, (hf + 1) * 1024)
                    P.dma("sp", x1h[:], xs[:, :, hsl], writes=["x1h"])
                    for c in range(NFC):
                        w = rW.next()
                        wreads = [(wname, r0) for r0 in range(0, D, 128)]
                        P.dma("sp", wg[w][:, :, 0:128], wsrc[:, :, c * 128:(c + 1) * 128], reads=wreads, writes=[("wg", w)])
                        P.dma("sp", wg[w][:, :, 128:256], wsrc[:, :, DFF + c * 128:DFF + (c + 1) * 128], reads=wreads, writes=[("wg", w)])
                        i = r2.next()
                        for (pt_, off, key) in ((pg[i], 0, ("pg", i)), (pu[i], 128, ("pu", i))):
                            for j in range(2):
                                for k in range(8):
                                    P.op("pe", lambda e, pt_=pt_, off=off, j=j, k=k: e.matmul(pt_[:, j * 512:(j + 1) * 512], wg[w][:, k, off:off + 128],
                                                                                           x1h[:, k, j * 512:(j + 1) * 512], start=(k == 0), stop=(k == 7)),
                                         reads=[("wg", w), "x1h"], writes=[key])
                        P.op("act", lambda e: e.activation(out=gc[i][:], in_=pg[i][:], func=AF.Identity, scale=cw[:, c, 2:3], bias=cb[:, c:c + 1]),
                             reads=[("pg", i), "cw", "cb"], writes=[("gc", i)])
                        P.op("dve", lambda e: e.scalar_tensor_tensor(out=gc[i][:, 1:1024], in0=pg[i][:, 0:1023], scalar=cw[:, c, 1:2], in1=gc[i][:, 1:1024],
                                                                     op0=ALU.mult, op1=ALU.add),
                             reads=[("pg", i), ("gc", i), "cw"], writes=[("gc", i)])
                        P.op("dve", lambda e: e.scalar_tensor_tensor(out=gc[i][:, 2:1024], in0=pg[i][:, 0:1022], scalar=cw[:, c, 0:1], in1=gc[i][:, 2:1024],
                                                                     op0=ALU.mult, op1=ALU.add),
                             reads=[("pg", i), ("gc", i), "cw"], writes=[("gc", i)])
                        if hf == 1:
                            P.op("dve", lambda e: e.scalar_tensor_tensor(out=gc[i][:, 0:2], in0=cprev[:, c, 0:2], scalar=cw[:, c, 0:1], in1=gc[i][:, 0:2],
                                                                         op0=ALU.mult, op1=ALU.add),
                                 reads=["cprev", ("gc", i), "cw"], writes=[("gc", i)])
                            P.op("dve", lambda e: e.scalar_tensor_tensor(out=gc[i][:, 0:1], in0=cprev[:, c, 1:2], scalar=cw[:, c, 1:2], in1=gc[i][:, 0:1],
                                                                         op0=ALU.mult, op1=ALU.add),
                                 reads=["cprev", ("gc", i), "cw"], writes=[("gc", i)])
                        else:
                            P.op("dve", lambda e: e.tensor_copy(out=cprev[:, c, :], in_=pg[i][:, 1022:1024]),
                                 reads=[("pg", i), ("gc", i)], writes=["cprev"])
                        P.op("act", lambda e: e.activation(out=gg[i][:], in_=gc[i][:], func=AF.Gelu), reads=[("gc", i)], writes=[("gg", i)])
                        P.op("dve", lambda e: e.tensor_tensor(out=fo[i][:], in0=gg[i][:], in1=pu[i][:], op=ALU.mult),
                             reads=[("gg", i), ("pu", i)], writes=[("fo", i)])
                        P.dma("sp", S["fT"][s, c * 128:(c + 1) * 128, hsl], fo[i][:], reads=[("fo", i)], writes=[("fT", s)])

    def phase_ffn_down(self, l, dst_tm, dst_fm):
        P, S = self.P, self.S
        with ExitStack() as st:
            Wd = self.load_w(st, "w_down%d" % l, DFF, D)
            Wg = self.load_w(st, "w_pg%d" % l, D, D)
            Wp = self.load_w(st, "w_pp%d" % l, 256, D)
            WDK, WGK, WPK = ("W_w_down%d" % l, 0), ("W_w_pg%d" % l, 0), ("W_w_pp%d" % l, 0)
            g_bc = self.bcast_row(st, "ln_g", self.small["lnfg"][l], D)
            b_bc = self.bcast_row(st, "ln_b", self.small["lnfb"][l], D)
            tl = self.ln_tiles(st)
            ft = [self.sb(st, "ft%d" % i, [128, NFC, 128], BF16) for i in range(2)]
            xr = [self.sb(st, "xr%d" % i, [128, D], F32) for i in range(2)]
            pb = [self.sb(st, "pb%d" % i, [128, 256], BF16) for i in range(2)]
            ptt = self.sb(st, "ptt", [128, 2, 128], BF16)
            sg = self.sb(st, "sg", [128, D], F32)
            x3 = self.sb(st, "x3t", [128, D], F32)
            pm = self.ps(st, "pm", [128, D], F32)
            pT2 = self.ps(st, "pT2", [128, 1024], BF16)
            pgate = self.ps(st, "pgate", [128, D], F32)
            ppp = self.ps(st, "ppp", [128, D], F32)
            for s in range(self.nseq):
                src = S["fT"][s].rearrange("(c p) t -> p c t", p=128)
                for t in range(NT):
                    b = t % 2
                    tsl = slice(t * 128, (t + 1) * 128)
                    P.dma("sp", ft[b][:], src[:, :, tsl], writes=[("ft", b)])
                    P.dma("sp", xr[b][:], S["x1"][s, tsl, :], writes=[("xr", b)])
                    P.dma("pool", pb[b][:], self.io["p"][l, s, tsl, :], writes=[("pb", b)])
                    for half in range(2):
                        for c in range(NFC):
                            P.op("pe", lambda e, c=c, half=half: e.matmul(pm[:, half * 512:(half + 1) * 512], ft[b][:, c, :],
                                                                          Wd[:, c, half * 512:(half + 1) * 512], start=(c == 0), stop=(c == NFC - 1)),
                                 reads=[("ft", b), WDK], writes=["pm"])
                    self.ln_epilogue(tl, pm, xr[b], ("xr", b), g_bc, b_bc, ["pm"], None, None, need_fm=True)
                    yn, xtt = tl["yn"], tl["xtt"]
                    for c in range(2):
                        P.op("pe", lambda e, c=c: e.transpose(pT2[:, c * 128:(c + 1) * 128], pb[b][:, c * 128:(c + 1) * 128], self.ident[:]),
                             reads=[("pb", b), "ident"], writes=["pT2"])
                    P.op("dve", lambda e: e.tensor_copy(out=ptt[:], in_=pT2[:, 0:256].rearrange("p (c t) -> p c t", c=2)), reads=["pT2"], writes=["ptt"])
                    for half in range(2):
                        hs = slice(half * 512, (half + 1) * 512)
                        for c in range(8):
                            P.op("pe", lambda e, c=c, hs=hs: e.matmul(pgate[:, hs], xtt[:, c, :], Wg[:, c, hs], start=(c == 0), stop=(c == 7)),
                                 reads=["ln_xtt", WGK], writes=["pgate"])
                        for c in range(2):
                            P.op("pe", lambda e, c=c, hs=hs: e.matmul(ppp[:, hs], ptt[:, c, :], Wp[:, c, hs], start=(c == 0), stop=(c == 1)),
                                 reads=["ptt", WPK], writes=["ppp"])
                    P.op("act", lambda e: e.activation(out=sg[:], in_=pgate[:], func=AF.Sigmoid), reads=["pgate"], writes=["sg"])
                    P.op("dve", lambda e: e.tensor_tensor(out=sg[:], in0=sg[:], in1=ppp[:], op=ALU.mult), reads=["sg", "ppp"], writes=["sg"])
                    P.op("pool", lambda e: e.tensor_tensor(out=x3[:], in0=sg[:], in1=yn[:], op=ALU.add), reads=["sg", "ln_yn"], writes=["x3t"])
                    P.dma("sp", dst_tm[s, tsl, :], x3[:], reads=["x3t"], writes=[("dst3", s)])
                    if dst_fm is not None:
                        self.to_fm(x3, "x3t", tl["ynb"], tl["xtt"], tl["pT"], dst_fm[s].rearrange("(c p) t -> p c t", p=128)[:, :, tsl])


    def phase_ssd_proj(self):
        P, S = self.P, self.S
        wsrc = self.wb["w_ssd_in"].rearrange("(k p) n -> p k n", p=128)
        wreads = [("w_ssd_in", r0) for r0 in range(0, D, 128)]
        for s in range(self.nseq):
            with ExitStack() as st:
                x3T = self.sb(st, "x3Tf", [128, 8, T], BF16)
                P.dma("sp", x3T[:], S["x3T"][s].rearrange("(c p) t -> p c t", p=128), writes=["x3Tf"])
                cw = self.sb(st, "scw", [128, 24, 4], F32)
                cb = self.sb(st, "scb", [128, 24], F32)
                for k in range(4):
                    P.dma("sp", cw[:, :, k], self.ssd_small["cw"][k].rearrange("(c p) -> p c", p=128), writes=["scw"], slow=True)
                P.dma("sp", cb[:], self.ssd_small["cb"].rearrange("(c p) -> p c", p=128), writes=["scb"], slow=True)
                with ExitStack() as st2:
                    wf = [self.sb(st2, "wf%d" % i, [128, 8, 128], BF16) for i in range(3)]
                    gc = [self.sb(st2, "sgc%d" % i, [128, T], F32) for i in range(2)]
                    go = [self.sb(st2, "sgo%d" % i, [128, T], BF16) for i in range(2)]
                    pF = [self.ps(st2, "pF%d" % i, [128, T], F32) for i in range(2)]
                    rW, r2 = Rot(3), Rot(2)
                    for cc in range(24):
                        w = rW.next()
                        P.dma("sp", wf[w][:], wsrc[:, :, 2048 + cc * 128:2048 + (cc + 1) * 128], reads=wreads, writes=[("wf", w)])
                        i = r2.next()
                        for j in range(4):
                            for k in range(8):
                                P.op("pe", lambda e, j=j, k=k: e.matmul(pF[i][:, j * 512:(j + 1) * 512], wf[w][:, k, :], x3T[:, k, j * 512:(j + 1) * 512],
                                                                       start=(k == 0), stop=(k == 7)),
                                     reads=[("wf", w), "x3Tf"], writes=[("pF", i)])
                        for hf in range(2):
                            P.op("act", lambda e, hf=hf: e.activation(out=gc[i][:, hf * 1024:(hf + 1) * 1024], in_=pF[i][:, hf * 1024:(hf + 1) * 1024],
                                                                      func=AF.Identity, scale=cw[:, cc, 3:4], bias=cb[:, cc:cc + 1]),
                                 reads=[("pF", i), "scw", "scb"], writes=[("sgc", i)])
                        for sh in (1, 2, 3):
                            for (o0, o1) in ((sh, 1024), (1024, T)):
                                P.op("dve", lambda e, sh=sh, o0=o0, o1=o1: e.scalar_tensor_tensor(out=gc[i][:, o0:o1], in0=pF[i][:, o0 - sh:o1 - sh],
                                                                                                scalar=cw[:, cc, 3 - sh:4 - sh], in1=gc[i][:, o0:o1],
                                                                                                op0=ALU.mult, op1=ALU.add),
                                     reads=[("pF", i), ("sgc", i), "scw"], writes=[("sgc", i)])
                        P.op("act", lambda e: e.activation(out=go[i][:], in_=gc[i][:], func=AF.Silu), reads=[("sgc", i)], writes=[("sgo", i)])
                        P.dma("sp", S["xbcT"][s, cc * 128:(cc + 1) * 128, :], go[i][:], reads=[("sgo", i)], writes=[("xbcT", s)])
                P.barrier()
                with ExitStack() as st2:
                    wz = self.sb(st2, "wz", [128, 8, 2048], BF16)
                    wdt = self.sb(st2, "wdt", [128, 8, 32], BF16)
                    P.dma("sp", wz[:], wsrc[:, :, 0:2048], reads=wreads, writes=["wz"])
                    P.dma("sp", wdt[:], wsrc[:, :, 5120:5152], reads=wreads, writes=["wdt"])
                    zo = [self.sb(st2, "zo%d" % i, [128, 2048], BF16) for i in range(2)]
                    dto = [self.sb(st2, "dto%d" % i, [128, 32], F32) for i in range(2)]
                    xin = [self.sb(st2, "xin%d" % i, [128, 20, 128], BF16) for i in range(2)]
                    xo = [self.sb(st2, "xo%d" % i, [128, 2560], BF16) for i in range(2)]
                    pZ = self.ps(st2, "pZ", [128, 2048], F32)
                    pDt = self.ps(st2, "pDt", [128, 512], F32)
                    pX = self.ps(st2, "pX", [128, 3072], BF16)
                    xsrc = S["xbcT"][s, 0:2560, :].rearrange("(c p) t -> p c t", p=128)
                    for t in range(NT):
                        b = t % 2
                        tsl = slice(t * 128, (t + 1) * 128)
                        P.dma("sp", xin[b][:], xsrc[:, :, tsl], reads=[("xbcT", s)], writes=[("xin", b)])
                        for j in range(4):
                            for k in range(8):
                                P.op("pe", lambda e, j=j, k=k: e.matmul(pZ[:, j * 512:(j + 1) * 512], x3T[:, k, tsl], wz[:, k, j * 512:(j + 1) * 512],
                                                                       start=(k == 0), stop=(k == 7)),
                                     reads=["wz", "x3Tf"], writes=["pZ"])
                        for k in range(8):
                            P.op("pe", lambda e, k=k: e.matmul(pDt[:, 0:32], x3T[:, k, tsl], wdt[:, k, :], start=(k == 0), stop=(k == 7)),
                                 reads=["wdt", "x3Tf"], writes=["pDt"])
                        for c in range(20):
                            P.op("pe", lambda e, c=c: e.transpose(pX[:, c * 128:(c + 1) * 128], xin[b][:, c, :], self.ident[:]),
                                 reads=[("xin", b), "ident"], writes=["pX"])
                        for hf in range(2):
                            P.op("act", lambda e, hf=hf: e.activation(out=zo[b][:, hf * 1024:(hf + 1) * 1024], in_=pZ[:, hf * 1024:(hf + 1) * 1024], func=AF.Silu),
                                 reads=["pZ"], writes=[("zo", b)])
                        P.op("dve", lambda e: e.tensor_copy(out=dto[b][:], in_=pDt[:, 0:32]), reads=["pDt"], writes=[("dto", b)])
                        for hf in range(2):
                            P.op("dve", lambda e, hf=hf: e.tensor_copy(out=xo[b][:, hf * 1280:(hf + 1) * 1280], in_=pX[:, hf * 1280:(hf + 1) * 1280]),
                                 reads=["pX"], writes=[("xo", b)])
                        P.dma("sp", S["zs_tm"][s, tsl, :], zo[b][:], reads=[("zo", b)], writes=[("zs_tm", s)])
                        P.dma("sp", S["dt_tm"][s, tsl, :], dto[b][:], reads=[("dto", b)], writes=[("dt_tm", s)])
                        P.dma("sp", S["xb_tm"][s, tsl, :], xo[b][:], reads=[("xo", b)], writes=[("xb_tm", s)])
                P.barrier()

    def phase_ssd_scan(self):
        P, S = self.P, self.S
        CD = self.consts_dram
        with ExitStack() as st:
            identf = self.sb(st, "identf", [128, 128], F32)
            U = self.sb(st, "Uc", [128, 128], F32)
            onesf = self.sb(st, "onesf", [128, 128], F32)
            neg = self.sb(st, "negm", [128, 128], F32)
            esel = self.sb(st, "esel", [32, 32, 128], F32)
            for (t_, k_) in ((identf, "identf"), (U, "U"), (onesf, "onesf"), (neg, "neg")):
                P.dma("sp", t_[:], CD[k_][:, :], writes=["c_" + k_])
            P.dma("sp", esel[:], CD["esel"][:, :, :], writes=["c_esel"])
            dtb = self.bcast_row(st, "dtb", self.ssd_small["dtb"], 32)
            alog = self.bcast_row(st, "alog", self.ssd_small["alog"], 32)
            dbc = self.bcast_row(st, "dbc", self.ssd_small["d"], 32)
            normw = self.bcast_row(st, "normw", self.ssd_small["norm"], 2048)
            dt = self.sb(st, "dt", [128, NT, 32], F32)
            da = self.sb(st, "da", [128, NT, 32], F32)
            la = self.sb(st, "la", [128, NT, 32], F32)
            lal = self.sb(st, "lal", [128, NT, 32], F32)
            ela = self.sb(st, "ela", [128, NT, 32], F32)
            dtw = self.sb(st, "dtw", [128, NT, 32], F32)
            elast = self.sb(st, "elast", [128, NT, 32], F32)
            laT = self.sb(st, "laT", [32, NT, 128], F32)
            st32 = [self.sb(st, "st32_%d" % g, [128, 512], F32) for g in range(4)]
            stb = [self.sb(st, "stb_%d" % g, [128, 512], BF16) for g in range(4)]
            xb = [self.sb(st, "xb%d" % i, [128, 2560], BF16) for i in range(2)]
            zs = [self.sb(st, "zs%d" % i, [128, 2048], BF16) for i in range(2)]
            BCT = [self.sb(st, "BCT%d" % i, [128, 8, 128], BF16) for i in range(2)]
            xdt = self.sb(st, "xdt", [128, 32, 64], BF16)
            xdtw = self.sb(st, "xdtw", [128, 32, 64], BF16)
            cbT = self.sb(st, "cbT", [128, 4, 128], BF16)
            seg = [self.sb(st, "seg%d" % i, [128, 512], F32) for i in range(2)]
            dec = [self.sb(st, "dec%d" % i, [128, 512], BF16) for i in range(2)]
            Mh = [self.sb(st, "Mh%d" % i, [128, 4, 128], BF16) for i in range(2)]
            ys = self.sb(st, "ys", [128, 512], F32)
            yt = self.sb(st, "yt", [128, 2048], F32)
            tmp2 = self.sb(st, "tmp2", [128, 2048], F32)
            junk = self.sb(st, "sjunk", [128, 512], F32)
            ss = self.sb(st, "ss", [128, 4], F32)
            yo = [self.sb(st, "yo%d" % i, [128, 2048], BF16) for i in range(2)]
            bview = lambda ap32, n: ap32.unsqueeze(2).to_broadcast([128, n, 64])
            for s in range(self.nseq):
                with ExitStack() as st2:
                    pL = self.ps(st2, "pL", [128, 512], F32)
                    pL2 = self.ps(st2, "pL2", [128, 512], F32)
                    pLT = self.ps(st2, "pLT", [32, NT * 128], F32)
                    P.dma("sp", dt[:], S["dt_tm"][s].rearrange("(c p) h -> p c h", p=128), reads=[("dt_tm", s)], writes=["dt"])
                    P.op("dve", lambda e: e.tensor_tensor(out=dt[:], in0=dt[:], in1=dtb[:].unsqueeze(1).to_broadcast([128, NT, 32]), op=ALU.add),
                         reads=["dt", "dtb"], writes=["dt"])
                    P.op("act", lambda e: e.activation(out=dt[:], in_=dt[:], func=AF.Exp), reads=["dt"], writes=["dt"])
                    P.op("act", lambda e: e.activation(out=dt[:], in_=dt[:], func=AF.Ln, bias=self.one1[:], scale=1.0), reads=["dt", "one1"], writes=["dt"])
                    if s == 0:
                        P.op("act", lambda e: e.activation(out=alog[:], in_=alog[:], func=AF.Exp), reads=["alog"], writes=["alog"])
                    P.op("dve", lambda e: e.scalar_tensor_tensor(out=da[:], in0=dt[:], scalar=-1.0, in1=alog[:].unsqueeze(1).to_broadcast([128, NT, 32]),
                                                                 op0=ALU.mult, op1=ALU.mult),
                         reads=["dt", "alog"], writes=["da"])
                    for ch in range(NT):
                        P.op("pe", lambda e, ch=ch: e.matmul(pL[:, ch * 32:(ch + 1) * 32], U[:], da[:, ch, :], start=True, stop=True),
                             reads=["c_U", "da"], writes=["pL"])
                        P.op("pe", lambda e, ch=ch: e.matmul(pL2[:, ch * 32:(ch + 1) * 32], onesf[:], da[:, ch, :], start=True, stop=True),
                             reads=["c_onesf", "da"], writes=["pL2"])
                    P.op("dve", lambda e: e.tensor_copy(out=la[:], in_=pL[:].rearrange("p (c h) -> p c h", c=NT)), reads=["pL"], writes=["la"])
                    P.op("dve", lambda e: e.tensor_copy(out=lal[:], in_=pL2[:].rearrange("p (c h) -> p c h", c=NT)), reads=["pL2"], writes=["lal"])
                    P.op("act", lambda e: e.activation(out=ela[:], in_=la[:], func=AF.Exp), reads=["la"], writes=["ela"])
                    P.op("act", lambda e: e.activation(out=elast[:], in_=lal[:], func=AF.Exp), reads=["lal"], writes=["elast"])
                    P.op("dve", lambda e: e.tensor_tensor(out=dtw[:], in0=lal[:], in1=la[:], op=ALU.subtract), reads=["lal", "la"], writes=["dtw"])
                    P.op("act", lambda e: e.activation(out=dtw[:], in_=dtw[:], func=AF.Exp), reads=["dtw"], writes=["dtw"])
                    P.op("dve", lambda e: e.tensor_tensor(out=dtw[:], in0=dtw[:], in1=dt[:], op=ALU.mult), reads=["dtw", "dt"], writes=["dtw"])
                    for ch in range(NT):
                        P.op("pe", lambda e, ch=ch: e.transpose(pLT[:, ch * 128:(ch + 1) * 128], la[:, ch, :], identf[:]),
                             reads=["la", "c_identf"], writes=["pLT"])
                    for hf in range(2):
                        P.op("dve", lambda e, hf=hf: e.tensor_copy(out=laT[:, hf * 8:(hf + 1) * 8, :],
                                                                   in_=pLT[:, hf * 1024:(hf + 1) * 1024].rearrange("p (c t) -> p c t", c=8)),
                             reads=["pLT"], writes=["laT"])
                    for g in range(4):
                        P.op("dve", lambda e, g=g: e.memset(st32[g][:], 0.0), writes=[("st32", g)])
                        P.op("dve", lambda e, g=g: e.memset(stb[g][:], 0.0), writes=[("stb", g)])
                    P.barrier()
                with ExitStack() as st2:
                    pC = self.ps(st2, "pC", [128, 512], F32)
                    pD = [self.ps(st2, "pD%d" % i, [128, 512], F32) for i in range(2)]
                    pY = [self.ps(st2, "pY%d" % i, [128, 512], F32) for i in range(2)]
                    pR = self.ps(st2, "pR", [128, 512], F32)
                    pSt = self.ps(st2, "pSt", [128, 512], F32)
                    rD, rY = Rot(2), Rot(2)
                    bcsrc = S["xbcT"][s, 2560:3072 + 512 - 512, :] if False else None
                    bct_src = S["xbcT"][s, 2048:3072, :].rearrange("(g p) t -> p g t", p=128)
                    for ch in range(NT):
                        b = ch % 2
                        tsl = slice(ch * 128, (ch + 1) * 128)
                        P.dma("sp", xb[b][:], S["xb_tm"][s, tsl, :], reads=[("xb_tm", s)], writes=[("xb", b)])
                        P.dma("sp", zs[b][:], S["zs_tm"][s, tsl, :], reads=[("zs_tm", s)], writes=[("zs", b)])
                        P.dma("sp", BCT[b][:], bct_src[:, :, tsl], reads=[("xbcT", s)], writes=[("BCT", b)])
                        xs3 = xb[b][:, 0:2048].rearrange("p (h d) -> p h d", h=32)
                        P.op("dve", lambda e: e.tensor_tensor(out=xdt[:], in0=xs3, in1=bview(dt[:, ch, :], 32), op=ALU.mult),
                             reads=[("xb", b), "dt"], writes=["xdt"])
                        P.op("dve", lambda e: e.tensor_tensor(out=xdtw[:], in0=xs3, in1=bview(dtw[:, ch, :], 32), op=ALU.mult),
                             reads=[("xb", b), "dtw"], writes=["xdtw"])
                        for g in range(4):
                            P.op("pe", lambda e, g=g: e.matmul(pC[:, g * 128:(g + 1) * 128], BCT[b][:, g, :], BCT[b][:, 4 + g, :], start=True, stop=True),
                                 reads=[("BCT", b)], writes=["pC"])
                        P.op("act", lambda e: e.copy(out=cbT[:], in_=pC[:].rearrange("p (g i) -> p g i", g=4)), reads=["pC"], writes=["cbT"])
                        for g in range(4):
                            y_i = rY.next()
                            for q2 in range(2):
                                q4 = g * 2 + q2
                                d_i = rD.next()
                                for hh in range(4):
                                    h = q4 * 4 + hh
                                    P.op("pe", lambda e, hh=hh, h=h: e.matmul(pD[d_i][:, hh * 128:(hh + 1) * 128], esel[:, h, :], laT[:, ch, :], start=True, stop=True),
                                         reads=["c_esel", "laT"], writes=[("pD", d_i)])
                                for hh in range(4):
                                    h = q4 * 4 + hh
                                    P.op("dve", lambda e, hh=hh, h=h: e.scalar_tensor_tensor(out=seg[d_i][:, hh * 128:(hh + 1) * 128], in0=pD[d_i][:, hh * 128:(hh + 1) * 128],
                                                                                            scalar=la[:, ch, h:h + 1], in1=neg[:], op0=ALU.subtract, op1=ALU.add),
                                         reads=[("pD", d_i), "la", "c_neg"], writes=[("seg", d_i)])
                                P.op("act", lambda e: e.activation(out=dec[d_i][:], in_=seg[d_i][:], func=AF.Exp), reads=[("seg", d_i)], writes=[("dec", d_i)])
                                for hh in range(4):
                                    P.op("pool", lambda e, hh=hh: e.tensor_tensor(out=Mh[d_i][:, hh, :], in0=dec[d_i][:, hh * 128:(hh + 1) * 128], in1=cbT[:, g, :], op=ALU.mult),
                                         reads=[("dec", d_i), "cbT"], writes=[("Mh", d_i)])
                                for hh in range(4):
                                    h = q4 * 4 + hh
                                    hl = h - 8 * g
                                    P.op("pe", lambda e, hh=hh, h=h, hl=hl: e.matmul(pY[y_i][:, hl * 64:(hl + 1) * 64], Mh[d_i][:, hh, :], xdt[:, h, :], start=True, stop=True),
                                         reads=[("Mh", d_i), "xdt"], writes=[("pY", y_i)])
                            P.op("pe", lambda e, g=g: e.matmul(pR[:], BCT[b][:, 4 + g, :], stb[g][:], start=True, stop=True),
                                 reads=[("BCT", b), ("stb", g)], writes=["pR"])
                            P.op("dve", lambda e, g=g: e.tensor_tensor(out=ys[:].rearrange("p (h d) -> p h d", h=8), in0=pR[:].rearrange("p (h d) -> p h d", h=8),
                                                                       in1=bview(ela[:, ch, 8 * g:8 * g + 8], 8), op=ALU.mult),
                                 reads=["pR", "ela"], writes=["ys"])
                            P.op("dve", lambda e, g=g: e.tensor_tensor(out=yt[:, g * 512:(g + 1) * 512], in0=pY[y_i][:], in1=ys[:], op=ALU.add),
                                 reads=[("pY", y_i), "ys"], writes=[("yt", g)])
                            P.op("pe", lambda e, g=g: e.matmul(pSt[:], xb[b][:, 2048 + g * 128:2048 + (g + 1) * 128],
                                                               xdtw[:, 8 * g:8 * g + 8, :].rearrange("p h d -> p (h d)"), start=True, stop=True),
                                 reads=[("xb", b), "xdtw"], writes=["pSt"])
                            P.op("dve", lambda e, g=g: e.tensor_tensor(out=st32[g][:].rearrange("p (h d) -> p h d", h=8), in0=st32[g][:].rearrange("p (h d) -> p h d", h=8),
                                                                       in1=bview(elast[:, ch, 8 * g:8 * g + 8], 8), op=ALU.mult),
                                 reads=[("st32", g), "elast"], writes=[("st32", g)])
                            P.op("dve", lambda e, g=g: e.tensor_tensor(out=st32[g][:], in0=st32[g][:], in1=pSt[:], op=ALU.add),
                                 reads=[("st32", g), "pSt"], writes=[("st32", g)])
                            P.op("act", lambda e, g=g: e.copy(out=stb[g][:], in_=st32[g][:]), reads=[("st32", g)], writes=[("stb", g)])
                        ytk = [("yt", g) for g in range(4)]
                        P.op("dve", lambda e: e.tensor_tensor(out=tmp2[:].rearrange("p (h d) -> p h d", h=32), in0=xs3, in1=bview(dbc[:, :], 32), op=ALU.mult),
                             reads=[("xb", b), "dbc"], writes=["tmp2"])
                        P.op("pool", lambda e: e.tensor_tensor(out=yt[:], in0=yt[:], in1=tmp2[:], op=ALU.add), reads=ytk + ["tmp2"], writes=ytk)
                        P.op("dve", lambda e: e.tensor_tensor(out=yt[:], in0=yt[:], in1=zs[b][:], op=ALU.mult), reads=ytk + [("zs", b)], writes=ytk)
                        P.op("dve", lambda e: e.memset(ss[:], 0.0), writes=["ss"])
                        for g in range(4):
                            P.op("act", lambda e, g=g: e.activation(out=junk[:], in_=yt[:, g * 512:(g + 1) * 512], func=AF.Square, accum_out=ss[:, g:g + 1]),
                                 reads=ytk + ["ss"], writes=["sjunk", "ss"])
                        P.op("act", lambda e: e.activation(out=ss[:], in_=ss[:], func=AF.Sqrt, bias=self.epsrms[:], scale=1.0 / 512), reads=["ss", "epsrms"], writes=["ss"])
                        P.op("dve", lambda e: e.reciprocal(out=ss[:], in_=ss[:]), reads=["ss"], writes=["ss"])
                        P.op("dve", lambda e: e.tensor_tensor(out=yt[:].rearrange("p (g c) -> p g c", g=4), in0=yt[:].rearrange("p (g c) -> p g c", g=4),
                                                              in1=ss[:, :].unsqueeze(2).to_broadcast([128, 4, 512]), op=ALU.mult),
                             reads=ytk + ["ss"], writes=ytk)
                        P.op("pool", lambda e: e.tensor_tensor(out=yo[b][:], in0=yt[:], in1=normw[:], op=ALU.mult), reads=ytk + ["normw"], writes=[("yo", b)])
                        P.dma("sp", S["yn_tm"][s, tsl, :], yo[b][:], reads=[("yo", b)], writes=[("yn_tm", s)])
                    P.barrier()


def make_in_maps(inputs, nseq=SPC, ncores=NCORES):
    w_in_perm, w_uq_perm, w_ukv_perm = _layer0_perms()
    c = _consts()
    f = lambda a: np.ascontiguousarray(np.asarray(a))
    shared = {
        "w_in0": f(np.asarray(inputs["att_w_in"])[0][:, w_in_perm]),
        "w_uq": f(np.asarray(inputs["mla_w_uq"])[0][:, w_uq_perm]),
        "w_ukv": f(np.asarray(inputs["mla_w_ukv"])[0][:, w_ukv_perm]),
        "w_out0": f(np.asarray(inputs["att_w_out"])[0]),
        "q_norm": f(np.asarray(inputs["mla_q_norm"])[0]),
        "kv_norm": f(np.asarray(inputs["mla_kv_norm"])[0]),
        "ln_mix_g": f(inputs["ln_mix_g"]), "ln_mix_b": f(inputs["ln_mix_b"]),
        "ln_ffn_g": f(inputs["ln_ffn_g"]), "ln_ffn_b": f(inputs["ln_ffn_b"]),
        "ffn_w_up": f(inputs["ffn_w_up"]), "ffn_conv_w": f(inputs["ffn_conv_w"]),
        "ffn_conv_b": f(inputs["ffn_conv_b"]), "ffn_w_down": f(inputs["ffn_w_down"]),
        "ple_w_gate": f(inputs["ple_w_gate"]), "ple_w_proj": f(inputs["ple_w_proj"]),
        "c_ident": c["ident"], "c_tri": c["tri"], "c_ones": c["ones"], "c_ropec": c["ropec"],
        "c_cdiff": c["cdiff"], "c_csum": c["csum"], "c_ek": c["ek"],
        "c_identf": c["identf"], "c_U": c["U"], "c_onesf": c["onesf"], "c_neg": c["neg"], "c_esel": c["esel"],
        "ssd_w_in": f(np.asarray(inputs["ssd_w_in"])[0]), "ssd_w_out": f(np.asarray(inputs["ssd_w_out"])[0]),
        "ssd_conv_w": f(np.asarray(inputs["ssd_conv_w"])[0]), "ssd_conv_b": f(np.asarray(inputs["ssd_conv_b"])[0]),
        "ssd_dt_bias": f(np.asarray(inputs["ssd_dt_bias"])[0]), "ssd_a_log": f(np.asarray(inputs["ssd_a_log"])[0]),
        "ssd_d": f(np.asarray(inputs["ssd_d"])[0]), "ssd_norm": f(np.asarray(inputs["ssd_norm"])[0]),
    }
    x = np.asarray(inputs["x"])
    p = np.asarray(inputs["p"])
    pos = np.asarray(inputs["positions"]).astype(np.int32)
    maps = []
    for i in range(ncores):
        b0 = i * nseq
        m = dict(shared)
        m["x"] = f(x[b0:b0 + nseq])
        m["p"] = f(p[:, b0:b0 + nseq])
        m["pos"] = f(pos[b0:b0 + nseq])
        maps.append(m)
    return maps


def kernel(**inputs):
    b = Builder()
    nc = b.build()
    maps = make_in_maps(inputs)
    res = run_bass_kernel_spmd(nc, maps, core_ids=list(range(NCORES)))
    outs = [np.asarray(r["out"]) for r in res.results]
    return np.concatenate(outs, axis=0).astype(np.float32)
```
